# Optimizing a Trainium2 kernel written in Bass

```python
import jax, jax.numpy as jnp
from jax import lax
import numpy as np

D_MODEL = 1024
BATCH = 4
SEQ = 4096
DEPTH = 4

HEAD_DIM = 64
GLA_WIDTH = 3 * D_MODEL // 8
HGRN_WIDTH = 3 * D_MODEL // 8
MLSTM_WIDTH = D_MODEL - GLA_WIDTH - HGRN_WIDTH
GLA_HEADS = GLA_WIDTH // HEAD_DIM
GLA_DK = HEAD_DIM // 2
GLA_GATE_RANK = 16
GLA_GATE_TAU = 16.0
HGRN_HEADS = HGRN_WIDTH // HEAD_DIM
HGRN_DK = HEAD_DIM
MLSTM_HEADS = MLSTM_WIDTH // HEAD_DIM
MLSTM_CONV = 4
D_FF = 4 * D_MODEL
CHUNK = 64
EPS = 1e-6
IN_SPLITS = (GLA_HEADS * GLA_DK, GLA_HEADS * GLA_DK, GLA_WIDTH, GLA_GATE_RANK, GLA_WIDTH,
             HGRN_HEADS * HGRN_DK, HGRN_HEADS * HGRN_DK, HGRN_WIDTH, HGRN_WIDTH,
             2 * MLSTM_WIDTH, MLSTM_WIDTH, MLSTM_HEADS, MLSTM_HEADS, MLSTM_WIDTH)
D_IN = sum(IN_SPLITS)
IN_OFFSETS = tuple(int(o) for o in np.cumsum(IN_SPLITS)[:-1])

kernel_name = 'hybrid_gla_hgrn2_mlstm_adaln_trunk'


def _rms(x):
    xf = x.astype(jnp.float32)
    return xf * lax.rsqrt(jnp.mean(xf * xf, axis=-1, keepdims=True) + EPS)


def modulated_rmsnorm(x, gain, shift, scale):
    y = _rms(x) * gain.astype(jnp.float32)
    y = y * (1.0 + scale[:, None]) + shift[:, None]
    return y.astype(x.dtype)


def split_heads(t, n_heads):
    return t.reshape(t.shape[:-1] + (n_heads, -1))


def head_rmsnorm(o, gain):
    return o * lax.rsqrt(jnp.mean(o * o, axis=-1, keepdims=True) + EPS) * gain.astype(jnp.float32)


def head_layernorm(h, gain):
    mu = jnp.mean(h, axis=-1, keepdims=True)
    hc = h - mu
    var = jnp.mean(hc * hc, axis=-1, keepdims=True)
    return hc * lax.rsqrt(var + EPS) * gain.astype(jnp.float32).reshape(h.shape[-2:])


def to_chunks(t):
    b, s = t.shape[:2]
    return jnp.moveaxis(t.reshape((b, s // CHUNK, CHUNK) + t.shape[2:]), 1, 0)


def gated_linear_attention(q, k, v, log_g):
    B, S, H, K = q.shape
    V = v.shape[-1]
    causal = jnp.tril(jnp.ones((CHUNK, CHUNK), bool))[None, :, :, None, None]

    def step(state, inp):
        qc, kc, vc, gc = inp
        b = jnp.cumsum(gc, axis=1)
        decay = jnp.exp(jnp.where(causal, b[:, :, None] - b[:, None, :], -jnp.inf))
        scores = jnp.einsum('bihk,bjhk,bijhk->bhij', qc, kc, decay)
        o = (jnp.einsum('bhij,bjhv->bihv', scores, vc)
             + jnp.einsum('bihk,bhkv->bihv', qc * jnp.exp(b), state))
        b_last = b[:, -1]
        state = (jnp.exp(b_last)[..., None] * state
                 + jnp.einsum('bjhk,bjhv->bhkv', kc * jnp.exp(b_last[:, None] - b), vc))
        return state, o

    f32 = jnp.float32
    xs = (to_chunks(q.astype(f32)), to_chunks(k.astype(f32)),
          to_chunks(v.astype(f32)), to_chunks(log_g.astype(f32)))
    _, o = lax.scan(step, jnp.zeros((B, H, K, V), f32), xs)
    return jnp.moveaxis(o, 0, 1).reshape(B, S, H, V)


def mlstm_chunkwise(q, k, v, log_i, log_f):
    B, S, H, Dh = q.shape
    N = S // CHUNK

    def blk(t):
        return jnp.moveaxis(t.reshape((B, N, CHUNK) + t.shape[2:]), 3, 1)

    q, k, v, log_i, log_f = blk(q), blk(k), blk(v), blk(log_i), blk(log_f)
    b = jnp.cumsum(log_f, axis=-1)
    causal = jnp.tril(jnp.ones((CHUNK, CHUNK), bool))
    log_w = jnp.where(causal, b[..., :, None] - b[..., None, :] + log_i[..., None, :], -jnp.inf)
    m_intra = jnp.max(log_w, axis=-1)
    b_last = b[..., -1]
    w_state = b_last[..., None] - b + log_i
    m_local = jnp.max(w_state, axis=-1)
    p_state = jnp.exp(w_state - m_local[..., None])
    u = jnp.einsum('bhnc,bhncd,bhnce->bhnde', p_state, k, v)
    nu = jnp.einsum('bhnc,bhncd->bhnd', p_state, k)

    def step(carry, inp):
        c_st, n_st, m = carry
        a_n, ml_n, u_n, nu_n = inp
        m_new = jnp.maximum(a_n + m, ml_n)
        s_old = jnp.exp(a_n + m - m_new)
        s_loc = jnp.exp(ml_n - m_new)
        c_new = s_old[..., None, None] * c_st + s_loc[..., None, None] * u_n
        n_new = s_old[..., None] * n_st + s_loc[..., None] * nu_n
        return (c_new, n_new, m_new), (c_st, n_st, m)

    f32 = jnp.float32
    init = (jnp.zeros((B, H, Dh, Dh), f32), jnp.zeros((B, H, Dh), f32), jnp.zeros((B, H), f32))
    xs = (jnp.moveaxis(b_last, 2, 0), jnp.moveaxis(m_local, 2, 0),
          jnp.moveaxis(u, 2, 0), jnp.moveaxis(nu, 2, 0))
    _, (c_prev, n_prev, m_prev) = lax.scan(step, init, xs)
    c_prev = jnp.moveaxis(c_prev, 0, 2)
    n_prev = jnp.moveaxis(n_prev, 0, 2)
    m_prev = jnp.moveaxis(m_prev, 0, 2)
    m_inter = b + m_prev[..., None]
    m_t = jnp.maximum(m_inter, m_intra)
    p = jnp.exp(log_w - m_t[..., None]) * jnp.einsum('bhnid,bhnjd->bhnij', q, k)
    s_inter = jnp.exp(m_inter - m_t)
    num = (jnp.einsum('bhnij,bhnje->bhnie', p, v)
           + s_inter[..., None] * jnp.einsum('bhnid,bhnde->bhnie', q, c_prev))
    den = jnp.sum(p, axis=-1) + s_inter * jnp.einsum('bhnid,bhnd->bhni', q, n_prev)
    h = num / jnp.maximum(jnp.abs(den), jnp.exp(-m_t))[..., None]
    return h.transpose(0, 2, 3, 1, 4).reshape(B, S, H, Dh)


def causal_depthwise_conv(x, w):
    return lax.conv_general_dilated(
        x, w[:, None, :].astype(x.dtype), window_strides=(1,), padding=[(MLSTM_CONV - 1, 0)],
        dimension_numbers=('NWC', 'WIO', 'NWC'), feature_group_count=x.shape[-1])


def gla_group(q, k, v, g_low, g_out, w_gate, b_gate, norm_gain):
    f32 = jnp.float32
    B, S, _ = q.shape
    q = split_heads(q.astype(f32), GLA_HEADS) * GLA_DK ** -0.5
    k = split_heads(k.astype(f32), GLA_HEADS)
    v = split_heads(v, GLA_HEADS)
    gate_logits = (g_low @ w_gate).astype(f32) + b_gate.astype(f32)
    log_g = split_heads(jax.nn.log_sigmoid(gate_logits) / GLA_GATE_TAU, GLA_HEADS)
    o = gated_linear_attention(q, k, v, log_g)
    o = head_rmsnorm(o, norm_gain) * jax.nn.silu(split_heads(g_out.astype(f32), GLA_HEADS))
    return o.reshape(B, S, GLA_WIDTH)


def hgrn2_group(q, f, i, g_out, lower_bound, norm_gain):
    f32 = jnp.float32
    B, S, _ = q.shape
    lb = lower_bound.reshape(HGRN_HEADS, HGRN_DK)
    f = split_heads(f.astype(f32), HGRN_HEADS)
    log_f = jnp.logaddexp(jnp.log(lb), jnp.log1p(-lb) + jax.nn.log_sigmoid(f))
    k = (1.0 - lb) * jax.nn.sigmoid(-f)
    q = jax.nn.silu(split_heads(q.astype(f32), HGRN_HEADS))
    o = gated_linear_attention(q, k, split_heads(i, HGRN_HEADS), log_f)
    o = head_rmsnorm(o, norm_gain) * jax.nn.sigmoid(split_heads(g_out.astype(f32), HGRN_HEADS))
    return o.reshape(B, S, HGRN_WIDTH)


def mlstm_group(qk_pre, v, i_pre, f_pre, g_out, conv_w, gate_b, norm_gain):
    f32 = jnp.float32
    B, S, _ = v.shape
    qk = jax.nn.silu(causal_depthwise_conv(qk_pre, conv_w).astype(f32))
    q, k = jnp.split(qk, 2, axis=-1)
    q = split_heads(q, MLSTM_HEADS)
    k = split_heads(k, MLSTM_HEADS) * HEAD_DIM ** -0.5
    gate_b = gate_b.astype(f32)
    log_i = i_pre.astype(f32) + gate_b[:MLSTM_HEADS]
    log_f = jax.nn.log_sigmoid(f_pre.astype(f32) + gate_b[MLSTM_HEADS:])
    h = mlstm_chunkwise(q, k, split_heads(v.astype(f32), MLSTM_HEADS), log_i, log_f)
    h = head_layernorm(h, norm_gain) * jax.nn.sigmoid(split_heads(g_out.astype(f32), MLSTM_HEADS))
    return h.reshape(B, S, MLSTM_WIDTH)


def setup_inputs(seed: int = 0) -> dict:
    key = jax.random.key(seed)
    ks = jax.random.split(key, 22)
    f32 = jnp.float32

    def nrm(k, shape, std):
        return jax.random.normal(k, shape, f32) * std

    forget_bias = jnp.linspace(3.0, 6.0, MLSTM_HEADS, dtype=f32)[None, :] + nrm(ks[15], (DEPTH, MLSTM_HEADS), 0.1)
    return {
        'x': nrm(ks[0], (BATCH, SEQ, D_MODEL), 1.0),
        'c': nrm(ks[1], (BATCH, D_MODEL), 1.0),
        'w_ada': nrm(ks[2], (DEPTH, D_MODEL, 6 * D_MODEL), 0.5 * D_MODEL ** -0.5),
        'b_ada': nrm(ks[3], (DEPTH, 6 * D_MODEL), 0.02),
        'norm_mix': 1.0 + nrm(ks[4], (DEPTH, D_MODEL), 0.02),
        'norm_mlp': 1.0 + nrm(ks[5], (DEPTH, D_MODEL), 0.02),
        'w_in': nrm(ks[6], (DEPTH, D_MODEL, D_IN), D_MODEL ** -0.5),
        'gla_w_gate': nrm(ks[7], (DEPTH, GLA_GATE_RANK, GLA_HEADS * GLA_DK), GLA_GATE_RANK ** -0.5),
        'gla_b_gate': nrm(ks[8], (DEPTH, GLA_HEADS * GLA_DK), 0.1),
        'gla_norm': 1.0 + nrm(ks[9], (DEPTH, HEAD_DIM), 0.02),
        'hgrn_lb_logits': nrm(ks[10], (DEPTH, HGRN_HEADS * HGRN_DK), 0.1),
        'hgrn_norm': 1.0 + nrm(ks[11], (DEPTH, HEAD_DIM), 0.02),
        'mlstm_conv': nrm(ks[12], (DEPTH, MLSTM_CONV, 2 * MLSTM_WIDTH), MLSTM_CONV ** -0.5),
        'mlstm_gate_b': jnp.concatenate([nrm(ks[14], (DEPTH, MLSTM_HEADS), 0.1), forget_bias], axis=-1),
        'mlstm_norm': 1.0 + nrm(ks[16], (DEPTH, MLSTM_WIDTH), 0.02),
        'w_out': nrm(ks[17], (DEPTH, D_MODEL, D_MODEL), D_MODEL ** -0.5),
        'w_ff1': nrm(ks[18], (DEPTH, D_MODEL, D_FF), D_MODEL ** -0.5),
        'w_ff2': nrm(ks[19], (DEPTH, D_FF, D_MODEL), D_FF ** -0.5),
        'final_norm': 1.0 + nrm(ks[20], (D_MODEL,), 0.02),
    }


def reference(x, c, w_ada, b_ada, norm_mix, norm_mlp, w_in, gla_w_gate, gla_b_gate, gla_norm,
              hgrn_lb_logits, hgrn_norm, mlstm_conv, mlstm_gate_b, mlstm_norm, w_out, w_ff1, w_ff2,
              final_norm):
    f32 = jnp.float32
    cond = jax.nn.silu(c.astype(f32))
    lb_cum = jnp.cumsum(jax.nn.softmax(hgrn_lb_logits.astype(f32), axis=0), axis=0)
    lower_bounds = lb_cum - lb_cum[0]
    for l in range(DEPTH):
        mod = cond @ w_ada[l].astype(f32) + b_ada[l].astype(f32)
        shift1, scale1, gate1, shift2, scale2, gate2 = jnp.split(mod, 6, axis=-1)
        h = modulated_rmsnorm(x, norm_mix[l], shift1, scale1)
        z = h @ w_in[l]
        (gq, gk, gv, glow, gout, hq, hf, hi, hout,
         mqk, mv, mi, mf, mout) = jnp.split(z, IN_OFFSETS, axis=-1)
        mixed = jnp.concatenate([
            gla_group(gq, gk, gv, glow, gout, gla_w_gate[l], gla_b_gate[l], gla_norm[l]),
            hgrn2_group(hq, hf, hi, hout, lower_bounds[l], hgrn_norm[l]),
            mlstm_group(mqk, mv, mi, mf, mout, mlstm_conv[l], mlstm_gate_b[l], mlstm_norm[l]),
        ], axis=-1).astype(x.dtype)
        x = x + gate1[:, None].astype(x.dtype) * (mixed @ w_out[l])
        h = modulated_rmsnorm(x, norm_mlp[l], shift2, scale2)
        ff = jnp.square(jax.nn.relu(h @ w_ff1[l])) @ w_ff2[l]
        x = x + gate2[:, None].astype(x.dtype) * ff
    return (_rms(x) * final_norm.astype(f32)).astype(x.dtype)
```

```python
from concourse.bass_utils import run_bass_kernel_spmd

import numpy as np
import concourse.bass as bass
import concourse.mybir as mybir

F32 = mybir.dt.float32
BF16 = mybir.dt.bfloat16
I32 = mybir.dt.int32
AF = mybir.ActivationFunctionType
from concourse.alu_op_type import AluOpType as ALU
AX = mybir.AxisListType


class Tok:
    __slots__ = ("name", "last_w", "readers", "excl")

    def __init__(self, name, excl=False):
        self.name = name
        self.last_w = None
        self.readers = []
        self.excl = excl


class Op:
    __slots__ = ("eng", "fn", "deps", "sig", "cnt", "dma_key", "waits", "gidx")

    def __init__(self, eng, fn, dma_key=None):
        self.eng = eng
        self.fn = fn
        self.deps = []
        self.sig = False
        self.cnt = 0
        self.dma_key = dma_key
        self.waits = []


ENGS = ("pe", "act", "dve", "pool", "sp")


class Prog:
    def __init__(self, nc):
        self.nc = nc
        self.ops = {e: [] for e in ENGS}
        self.n = 0
        self.dma_keys = []

    def add(self, eng, fn, reads=(), writes=(), dma_key=None):
        op = Op(eng, fn, dma_key)
        op.gidx = self.n
        self.n += 1
        if dma_key is not None and dma_key not in self.dma_keys:
            self.dma_keys.append(dma_key)
        deps = []
        for t in reads:
            if t.excl:
                if t.last_w is not None:
                    deps.append(t.last_w)
                deps.extend(t.readers)
                t.last_w = op
                t.readers = []
            else:
                if t.last_w is not None:
                    deps.append(t.last_w)
                t.readers.append(op)
        for t in writes:
            if t.last_w is not None:
                deps.append(t.last_w)
            deps.extend(r for r in t.readers if r is not op)
            t.last_w = op
            t.readers = []
        seen = set()
        for d in deps:
            if d is op or id(d) in seen:
                continue
            seen.add(id(d))
            if d.eng == "pe" and eng == "pe" and d.dma_key is None and dma_key is None:
                continue
            op.deps.append(d)
        self.ops[eng].append(op)
        return op

    def finalize_and_emit(self):
        nc = self.nc
        for e in ENGS:
            for op in self.ops[e]:
                if op.dma_key is not None:
                    op.sig = True
                for d in op.deps:
                    d.sig = True
        dma_cnt = {k: 0 for k in self.dma_keys}
        for e in ENGS:
            c = 0
            for op in self.ops[e]:
                if op.dma_key is not None:
                    if op.sig:
                        dma_cnt[op.dma_key] += 16
                        op.cnt = dma_cnt[op.dma_key]
                elif op.sig:
                    c += 1
                    op.cnt = c
        for e in ENGS:
            waited = {}
            for op in self.ops[e]:
                need = {}
                for d in op.deps:
                    key = ("dma", d.dma_key) if d.dma_key is not None else ("eng", d.eng)
                    if d.cnt > need.get(key, 0):
                        need[key] = d.cnt
                for key, v in need.items():
                    if waited.get(key, 0) >= v:
                        continue
                    waited[key] = v
                    op.waits.append((key, v))
        import contextlib
        with contextlib.ExitStack() as es:
            sems = {}
            for e in ENGS:
                sems[("eng", e)] = es.enter_context(nc.semaphore("s_" + e))
            for k in self.dma_keys:
                sems[("dma", k)] = es.enter_context(nc.semaphore("d_" + str(k)))
            block = es.enter_context(nc.Block())
            ops = self.ops

            def run(eng_obj, e):
                for op in ops[e]:
                    for key, v in op.waits:
                        eng_obj.wait_ge(sems[key], v)
                    ins = op.fn(eng_obj)
                    if op.sig:
                        if op.dma_key is not None:
                            ins.then_inc(sems[("dma", op.dma_key)], 16)
                        else:
                            ins.then_inc(sems[("eng", e)], 1)

            @block.tensor
            def _(eng):
                run(eng, "pe")

            @block.scalar
            def _(eng):
                run(eng, "act")

            @block.vector
            def _(eng):
                run(eng, "dve")

            @block.gpsimd
            def _(eng):
                run(eng, "pool")

            @block.sync
            def _(eng):
                run(eng, "sp")

D = 1024
SEQ = 4096
NL = 4
DFF = 4096
TT_ = 256
NT = SEQ // TT_
EPS = 1e-6
NFC = 1688
NTC = 2048
LN_HALF = float(np.log(0.5))

FCH = {}
_o = 0
for _n, _w in [("gq0", 96), ("gq1", 96), ("gk0", 96), ("gk1", 96), ("glow", 16),
               ("hq0", 128), ("hq1", 128), ("hq2", 128), ("hf0", 128), ("hf1", 128), ("hf2", 128),
               ("mq0", 128), ("mq1", 128), ("mk0", 128), ("mk1", 128), ("mimf", 8)]:
    FCH[_n] = (_o, _w)
    _o += _w
assert _o == NFC


def host_prep(inputs):
    f32 = np.float32
    w_in = np.asarray(inputs["w_in"], f32)
    gq, gk, gv, glow, gout = (0, 192), (192, 384), (384, 768), (768, 784), (784, 1168)
    hq, hf, hi, hout = (1168, 1552), (1552, 1936), (1936, 2320), (2320, 2704)
    mqk, mv, mi, mf, mout = (2704, 3216), (3216, 3472), (3472, 3476), (3476, 3480), (3480, 3736)
    cols = []
    cols += list(range(gq[0], gq[1])) + list(range(gk[0], gk[1])) + list(range(glow[0], glow[1]))
    cols += list(range(hq[0], hq[1])) + list(range(hf[0], hf[1]))
    cols += list(range(mqk[0], mqk[1])) + list(range(mi[0], mi[1])) + list(range(mf[0], mf[1]))
    assert len(cols) == NFC
    cols += list(range(gv[0], gv[1])) + list(range(hi[0], hi[1])) + list(range(mv[0], mv[1]))
    cols += list(range(gout[0], gout[1])) + list(range(hout[0], hout[1])) + list(range(mout[0], mout[1]))
    assert len(cols) == NFC + NTC
    w_in_re = np.ascontiguousarray(w_in[:, :, cols])

    def fmaj(v, nchunk):
        v = np.asarray(v, f32)
        return np.ascontiguousarray(np.swapaxes(v.reshape(v.shape[:-1] + (nchunk, 128)), -1, -2))

    com = {}
    com["w_in"] = w_in_re
    com["w_ada"] = np.asarray(inputs["w_ada"], f32)
    com["w_out"] = np.asarray(inputs["w_out"], f32)
    com["w_ff1"] = np.asarray(inputs["w_ff1"], f32)
    com["w_ff2"] = np.asarray(inputs["w_ff2"], f32)
    com["b_ada"] = fmaj(inputs["b_ada"], 48)
    com["nmix"] = fmaj(inputs["norm_mix"], 8)
    com["nmlp"] = fmaj(inputs["norm_mlp"], 8)
    com["wgate"] = np.asarray(inputs["gla_w_gate"], f32)
    bg = np.asarray(inputs["gla_b_gate"], f32)
    com["bgate"] = np.ascontiguousarray(np.swapaxes(bg.reshape(NL, 2, 96), 1, 2))
    lbl = np.asarray(inputs["hgrn_lb_logits"], f32)
    com["lblog"] = np.ascontiguousarray(np.transpose(lbl.reshape(NL, 3, 128), (2, 0, 1)))
    gn = np.asarray(inputs["gla_norm"], f32)
    hn = np.asarray(inputs["hgrn_norm"], f32)
    mn = np.asarray(inputs["mlstm_norm"], f32)
    com["gainrow"] = np.ascontiguousarray(np.concatenate([np.tile(gn, (1, 6)), np.tile(hn, (1, 6)), mn], axis=1))
    cw = np.asarray(inputs["mlstm_conv"], f32)
    com["convw"] = np.ascontiguousarray(np.transpose(cw.reshape(NL, 4, 4, 128), (0, 3, 2, 1)))
    com["mgb"] = np.ascontiguousarray(np.asarray(inputs["mlstm_gate_b"], f32).reshape(NL, 8, 1))
    com["fnorm"] = np.ascontiguousarray(np.asarray(inputs["final_norm"], f32).reshape(1, D))
    com["ident"] = np.eye(128, dtype=f32)
    p = np.arange(128)[:, None]
    c = np.arange(192)[None, :]
    com["maskp"] = ((p % 64) <= (c % 64)).astype(f32)
    rm = np.ones((128, TT_), f32)
    rm[:, ::64] = 0.0
    com["rmask"] = rm
    c192 = np.arange(192)[None, :]
    com["bm_gla"] = ((p // 32) == (c192 // 64)).astype(f32)[:, :192]
    c128 = np.arange(128)[None, :]
    com["bm_hg"] = ((p // 64) == (c128 // 64)).astype(f32)
    c130 = np.arange(130)[None, :]
    com["bm_ml"] = ((p // 64) == (c130 // 65)).astype(f32)
    sel = np.zeros((8, 6, 128), f32)
    for t in range(2):
        for m in range(128):
            sel[4 + 2 * t + m // 64, t, m] = -1.0
            sel[4 + 2 * t + m // 64, 2 + t, m] = 1.0
            sel[2 * t + m // 64, 4 + t, m] = 1.0
    com["sel"] = sel
    return com


def build_program(nlayers=NL, debug_out=None):
    nc = bass.Bass("TRN2", target_bir_lowering=False)
    dr = {}

    def din(name, shape):
        dr[name] = nc.dram_tensor(name, list(shape), F32, kind="ExternalInput").ap()
        return dr[name]

    x_d = din("x", [SEQ, D])
    cT_d = din("cT", [128, 8])
    w_in_d = din("w_in", [NL, D, NFC + NTC])
    w_ada_d = din("w_ada", [NL, D, 6 * D])
    w_out_d = din("w_out", [NL, D, D])
    w_ff1_d = din("w_ff1", [NL, D, DFF])
    w_ff2_d = din("w_ff2", [NL, DFF, D])
    b_ada_d = din("b_ada", [NL, 128, 48])
    nmix_d = din("nmix", [NL, 128, 8])
    nmlp_d = din("nmlp", [NL, 128, 8])
    wgate_d = din("wgate", [NL, 16, 192])
    bgate_d = din("bgate", [NL, 96, 2])
    lblog_d = din("lblog", [128, NL, 3])
    gainrow_d = din("gainrow", [NL, D])
    convw_d = din("convw", [NL, 128, 4, 4])
    mgb_d = din("mgb", [NL, 8, 1])
    fnorm_d = din("fnorm", [1, D])
    ident_d = din("ident", [128, 128])
    maskp_d = din("maskp", [128, 192])
    rmask_d = din("rmask", [128, TT_])
    bm_gla_d = din("bm_gla", [128, 192])
    bm_hg_d = din("bm_hg", [128, 128])
    bm_ml_d = din("bm_ml", [128, 130])
    sel_d = din("sel", [8, 6, 128])
    out_d = nc.dram_tensor("out", [SEQ, D], F32, kind="ExternalOutput").ap()
    xa_d = nc.dram_tensor("xa_scr", [SEQ, D], F32).ap()
    xb_d = nc.dram_tensor("xb_scr", [SEQ, D], F32).ap()

    import contextlib
    es = contextlib.ExitStack()
    with es:
        P = Prog(nc)
        toks = {}

        def tk(name, excl=False):
            if name not in toks:
                toks[name] = Tok(name, excl)
            return toks[name]

        def sb(name, cols, dt=F32, parts=128):
            t = es.enter_context(nc.sbuf_tensor("s_" + name, [parts, cols], dt))
            tk(name)
            return t

        pmm = [es.enter_context(nc.psum_tensor("pmm%d" % i, [128, 512], F32)) for i in range(3)]
        ptr = es.enter_context(nc.psum_tensor("ptr", [128, 1024], BF16))
        pP = es.enter_context(nc.psum_tensor("pP", [128, 512], F32))
        pO = es.enter_context(nc.psum_tensor("pO", [128, 512], F32))
        pU = es.enter_context(nc.psum_tensor("pU", [128, 512], F32))
        pX = es.enter_context(nc.psum_tensor("pX", [128, 512], F32))
        for n in ["pmm0", "pmm1", "pmm2", "ptr", "pP", "pO", "pU", "pX"]:
            tk(n, excl=True)
        mm_rr = [0]

        def next_mm():
            i = mm_rr[0] % 3
            mm_rr[0] += 1
            return pmm[i], tk("pmm%d" % i)

        WAR_COLS = 38080
        warena = sb("warena", WAR_COLS, BF16)
        wF = warena[:, 0:8 * NFC].rearrange("p (k c) -> p k c", k=8)
        wT = warena[:, 8 * NFC:8 * NFC + 8 * NTC].rearrange("p (k c) -> p k c", k=8)
        wO = warena[:, 8 * (NFC + NTC):8 * (NFC + NTC) + 8192].rearrange("p (k c) -> p k c", k=8)
        w1h = warena[:, 0:16384].rearrange("p (k c) -> p k c", k=8)
        w2h = warena[:, 16384:32768].rearrange("p (k c) -> p k c", k=16)
        ARENA = tk("ARENA")

        ident_f = sb("ident_f", 128)
        ident_b = sb("ident_b", 128, BF16)
        ones_f = sb("ones_f", 128)
        maskp_f = sb("maskp_f", 192)
        maskp = sb("maskp", 192, mybir.dt.uint8)
        rmask = sb("rmask", TT_)
        bm_gla = sb("bm_gla", 192)
        bm_hg = sb("bm_hg", 128)
        bm_ml = sb("bm_ml", 130)
        sel = sb("sel", 6 * 128)
        cst = sb("cst", 8)
        cond = sb("cond", 8)
        ctmp = sb("ctmp", 8)
        lbw = sb("lbw", 3 * NL * 4)
        hsc = sb("hsc", NL * 3)
        hbi = sb("hbi", NL * 3)
        hnsc = sb("hnsc", NL * 3)
        wst = [sb("wst%d" % i, 8 * 256) for i in range(2)]
        modsb = sb("modsb", 48)
        bada = sb("bada", 48)
        nmix = sb("nmix", 8)
        nmlp = sb("nmlp", 8)
        g1 = sb("g1", 8)
        g2 = sb("g2", 8)
        gate1_bc = sb("gate1_bc", D)
        gate2_bc = sb("gate2_bc", D)
        gain_h = sb("gain_h", D)
        diag = [sb("diag%d" % i, 128) for i in range(2)]
        wgate_f = sb("wgate_f", 192)
        wgate = sb("wgate", 192, BF16)
        nbgate = sb("nbgate", 2)
        convw = sb("convw", 16)
        mgb = sb("mgb", 1)
        nmgb = sb("nmgb", 1)

        xt = sb("xt", 2 * D)
        xacc = sb("xacc", 2 * D)
        xn = sb("xn", 2 * D, BF16)
        hT = sb("hT", 8 * TT_, BF16)
        Vsb = sb("Vsb", 2 * 1028, BF16)
        gts = sb("gts", 2 * D, BF16)
        o_sb = sb("o_sb", 2 * 1040)
        bufA = sb("bufA", D)
        bufB = sb("bufB", D)
        tmpY = sb("tmpY", D)
        stat = sb("stat", 64)
        gF = sb("gF", TT_)
        bF = sb("bF", TT_)
        dF = sb("dF", TT_)
        E1 = sb("E1", TT_)
        E2 = sb("E2", TT_)
        th = sb("th", TT_)
        ksrc = sb("ksrc", TT_)
        qsrc = sb("qsrc", TT_)
        glowT = sb("glowT", TT_, BF16)
        qT = [sb("qT%d" % i, TT_, BF16) for i in range(2)]
        kT = [sb("kT%d" % i, TT_, BF16) for i in range(2)]
        ktok = [sb("ktok%d" % i, 2 * 128, BF16) for i in range(2)]
        qp = [sb("qp%d" % i, TT_, BF16) for i in range(3)]
        csc = [sb("csc%d" % i, 12) for i in range(2)]
        cs8 = sb("cs8", 12)
        x8 = sb("x8", TT_)
        l8 = sb("l8", TT_)
        b8 = sb("b8", TT_)
        d8 = sb("d8", TT_)
        convb = [sb("convb%d" % i, TT_ + 3) for i in range(4)]
        S = {}
        for n_, w_ in [("g0", 192), ("g1", 192), ("h0", 128), ("h1", 128), ("h2", 128), ("m0", 130), ("m1", 130)]:
            S[n_] = sb("S_" + n_, w_)
        Sr = [sb("Sr%d" % i, 192, BF16) for i in range(2)]
        PT = [sb("PT%d" % i, 192, BF16) for i in range(2)]

        rr = {"ft": 0, "sr": 0, "pt": 0, "wst": 0, "diag": 0}

        def T_(*names):
            return [tk(n) for n in names]

        def DMA(eng, out, in_, r, w, key):
            return P.add(eng, lambda e: e.dma_start(out=out, in_=in_), r, w, dma_key=key)

        def MM(out, lhsT, rhs, start, stop, r, w):
            return P.add("pe", lambda e: e.matmul(out, lhsT=lhsT, rhs=rhs, start=start, stop=stop), r, w)

        def TR(out, in_, r, w):
            return P.add("pe", lambda e: e.transpose(out=out, in_=in_, identity=ident_b[0:in_.shape[0], 0:in_.shape[0]]), r + [tk("ident_b")], w)

        def ACT(out, in_, func, r, w, scale=1.0, bias=0.0, accum=None):
            if accum is None:
                return P.add("act", lambda e: e.activation(out=out, in_=in_, func=func, scale=scale, bias=bias), r, w)
            return P.add("act", lambda e: e.activation(out=out, in_=in_, func=func, scale=scale, bias=bias, accum_out=accum), r, w)

        def TTo(eng, out, in0, in1, op, r, w):
            return P.add(eng, lambda e: e.tensor_tensor(out=out, in0=in0, in1=in1, op=op), r, w)

        def TS(eng, out, in0, s1, s2, op0, op1, r, w):
            if op1 is None:
                return P.add(eng, lambda e: e.tensor_scalar(out=out, in0=in0, scalar1=s1, scalar2=None, op0=op0), r, w)
            return P.add(eng, lambda e: e.tensor_scalar(out=out, in0=in0, scalar1=s1, scalar2=s2, op0=op0, op1=op1), r, w)

        def STT(out, in0, scalar, in1, op0, op1, r, w):
            return P.add("dve", lambda e: e.scalar_tensor_tensor(out=out, in0=in0, scalar=scalar, in1=in1, op0=op0, op1=op1), r, w)

        def CP(eng, out, in_, r, w):
            if eng == "act":
                return P.add("act", lambda e: e.copy(out=out, in_=in_), r, w)
            return P.add(eng, lambda e: e.tensor_copy(out=out, in_=in_), r, w)

        def MS(eng, ap, val, w):
            return P.add(eng, lambda e: e.memset(ap, val), [], w)

        def RED(out, in_, r, w):
            return P.add("dve", lambda e: e.tensor_reduce(out=out, in_=in_, axis=AX.X, op=ALU.add), r, w)

        def ld(dst, src, name):
            DMA("sp", dst, src, [], T_(name), "c_" + name)

        ld(ident_f[:], ident_d[:], "ident_f")
        ld(maskp_f[:], maskp_d[:], "maskp_f")
        ld(rmask[:], rmask_d[:], "rmask")
        ld(bm_gla[:], bm_gla_d[:], "bm_gla")
        ld(bm_hg[:], bm_hg_d[:], "bm_hg")
        ld(bm_ml[:], bm_ml_d[:], "bm_ml")
        ld(sel[0:8, :], sel_d[:].rearrange("r s m -> r (s m)"), "sel")
        ld(ctmp[:], cT_d[:], "ctmp")
        ld(lbw[:, 0:NL * 3], lblog_d[:].rearrange("p l t -> p (l t)"), "lbw")
        CP("dve", ident_b[:], ident_f[:], T_("ident_f"), T_("ident_b"))
        CP("dve", maskp[:], maskp_f[:], T_("maskp_f"), T_("maskp"))
        MS("pool", ones_f[:], 1.0, T_("ones_f"))
        MS("pool", cst[:, 0:1], 1.0, T_("cst"))
        MS("pool", cst[:, 1:2], EPS, T_("cst"))
        MS("pool", cst[:, 2:3], LN_HALF, T_("cst"))
        MS("pool", cst[:, 3:4], LN_HALF + float(np.log(0.125)), T_("cst"))
        MS("pool", cst[:, 4:5], float(np.log(32.0 ** -0.5)), T_("cst"))
        MS("pool", cst[:, 5:6], 0.0, T_("cst"))
        for i in range(2):
            MS("pool", PT[i][:], 0.0, T_("PT%d" % i))
            MS("pool", Vsb[:], 1.0, T_("Vsb"))
        C_ONE, C_EPS, C_LNH, C_LNK, C_LNQ, C_ZERO = [cst[:, i:i + 1] for i in range(6)]
        ACT(cond[:], ctmp[:], AF.Tanh, T_("ctmp"), T_("cond"), scale=0.5)
        STT(cond[:], cond[:], 1.0, ctmp[:], ALU.add, ALU.mult, T_("cond", "ctmp"), T_("cond"))
        TS("dve", cond[:], cond[:], 0.5, None, ALU.mult, None, T_("cond"), T_("cond"))
        lg = lbw[:, 0:12].rearrange("p (l t) -> p l t", l=NL)
        mx = lbw[:, 12:15]
        ex = lbw[:, 16:28].rearrange("p (l t) -> p l t", l=NL)
        sm = lbw[:, 28:31]
        lbv = lbw[:, 32:44].rearrange("p (l t) -> p l t", l=NL)
        TTo("dve", mx, lg[:, 0, :], lg[:, 1, :], ALU.max, T_("lbw"), T_("lbw"))
        TTo("dve", mx, mx, lg[:, 2, :], ALU.max, T_("lbw"), T_("lbw"))
        TTo("dve", mx, mx, lg[:, 3, :], ALU.max, T_("lbw"), T_("lbw"))
        for l in range(NL):
            TTo("dve", ex[:, l, :], lg[:, l, :], mx, ALU.subtract, T_("lbw"), T_("lbw"))
        ACT(lbw[:, 16:28], lbw[:, 16:28], AF.Exp, T_("lbw"), T_("lbw"))
        TTo("dve", sm, ex[:, 0, :], ex[:, 1, :], ALU.add, T_("lbw"), T_("lbw"))
        TTo("dve", sm, sm, ex[:, 2, :], ALU.add, T_("lbw"), T_("lbw"))
        TTo("dve", sm, sm, ex[:, 3, :], ALU.add, T_("lbw"), T_("lbw"))
        P.add("dve", lambda e: e.reciprocal(out=sm, in_=sm), T_("lbw"), T_("lbw"))
        for l in range(NL):
            TTo("dve", ex[:, l, :], ex[:, l, :], sm, ALU.mult, T_("lbw"), T_("lbw"))
        MS("dve", lbv[:, 0, :], 0.0, T_("lbw"))
        for l in range(1, NL):
            TTo("dve", lbv[:, l, :], lbv[:, l - 1, :], ex[:, l, :], ALU.add, T_("lbw"), T_("lbw"))
        TS("dve", hsc[:], lbw[:, 32:44], -0.5, 0.5, ALU.mult, ALU.add, T_("lbw"), T_("hsc"))
        TS("dve", hbi[:], lbw[:, 32:44], 0.5, 0.5, ALU.mult, ALU.add, T_("lbw"), T_("hbi"))
        TS("dve", hnsc[:], lbw[:, 32:44], 0.5, -0.5, ALU.mult, ALU.add, T_("lbw"), T_("hnsc"))

        def xtok(which, t):
            return tk("%s_t%d" % (which, t))

        def tile_rows(ap, t):
            return ap[t * TT_:(t + 1) * TT_, :].rearrange("(b p) d -> p b d", p=128)

        xt3 = xt[:].rearrange("p (b d) -> p b d", b=2)
        xacc3 = xacc[:].rearrange("p (b d) -> p b d", b=2)
        xn3 = xn[:].rearrange("p (b d) -> p b d", b=2)
        hT3 = hT[:].rearrange("p (k t) -> p k t", k=8)
        V3 = Vsb[:].rearrange("p (b c) -> p b c", b=2)
        g3 = gts[:].rearrange("p (b d) -> p b d", b=2)
        o3 = o_sb[:].rearrange("p (b c) -> p b c", b=2)
        ptr3 = ptr[:].rearrange("p (k t) -> p k t", k=8)

        def ada_ln(l):
            DMA("sp", bada[:], b_ada_d[l], [], T_("bada"), "c_bada")
            DMA("sp", nmix[:], nmix_d[l], [], T_("nmix"), "c_nmix")
            DMA("sp", nmlp[:], nmlp_d[l], [], T_("nmlp"), "c_nmlp")
            for s in range(24):
                i = rr["wst"] % 2
                rr["wst"] += 1
                wv = wst[i][:].rearrange("p (k c) -> p k c", k=8)
                DMA("sp", wv, w_ada_d[l][:, s * 256:(s + 1) * 256].rearrange("(k p) c -> p k c", p=128), [], T_("wst%d" % i), "wst%d" % i)
                for j in range(2):
                    cc = s * 2 + j
                    for k in range(8):
                        MM(pX[:, cc:cc + 1], wv[:, k, j * 128:(j + 1) * 128], cond[:, k:k + 1], k == 0, k == 7,
                           T_("wst%d" % i, "cond"), T_("pX"))
                TTo("dve", modsb[:, s * 2:s * 2 + 2], pX[:, s * 2:s * 2 + 2], bada[:, s * 2:s * 2 + 2], ALU.add, T_("pX", "bada"), T_("modsb"))
            STT(g1[:], modsb[:, 8:16], 1.0, nmix[:], ALU.add, ALU.mult, T_("modsb", "nmix"), T_("g1"))
            STT(g2[:], modsb[:, 32:40], 1.0, nmlp[:], ALU.add, ALU.mult, T_("modsb", "nmlp"), T_("g2"))
            for which, base, dst, dname in ((0, 16, gate1_bc, "gate1_bc"), (1, 40, gate2_bc, "gate2_bc")):
                for half in range(2):
                    for f4 in range(4):
                        fc = half * 4 + f4
                        i = rr["diag"] % 2
                        rr["diag"] += 1
                        TS("dve", diag[i][:], ident_f[:], modsb[:, base + fc:base + fc + 1], None, ALU.mult, None,
                           T_("ident_f", "modsb"), T_("diag%d" % i))
                        MM(pX[:, f4 * 128:(f4 + 1) * 128], ones_f[:], diag[i][:], True, True, T_("ones_f", "diag%d" % i), T_("pX"))
                    CP("dve", dst[:, half * 512:(half + 1) * 512], pX[:, 0:512], T_("pX"), T_(dname))

        def layer_small(l):
            DMA("sp", gain_h[:], gainrow_d[l:l + 1, :].partition_broadcast(128), [], T_("gain_h"), "c_gain")
            TS("pool", gain_h[:], gain_h[:], 0.5, None, ALU.mult, None, T_("gain_h"), T_("gain_h"))
            DMA("sp", wgate_f[0:16, :], wgate_d[l], [], T_("wgate_f"), "c_wgate")
            CP("dve", wgate[0:16, :], wgate_f[0:16, :], T_("wgate_f"), T_("wgate"))
            DMA("sp", nbgate[0:96, :], bgate_d[l], [], T_("nbgate"), "c_bgate")
            TS("dve", nbgate[0:96, :], nbgate[0:96, :], -1.0, None, ALU.mult, None, T_("nbgate"), T_("nbgate"))
            DMA("sp", convw[:], convw_d[l].rearrange("p c k -> p (c k)"), [], T_("convw"), "c_convw")
            DMA("sp", mgb[0:8, :], mgb_d[l], [], T_("mgb"), "c_mgb")
            TS("dve", nmgb[0:8, :], mgb[0:8, :], -1.0, None, ALU.mult, None, T_("mgb"), T_("nmgb"))
            for n_ in S:
                MS("pool", S[n_][:], 0.0, T_("S_" + n_))
            for i in range(4):
                MS("pool", convb[i][:, 0:3], 0.0, T_("convb%d" % i))

        def fence():
            P.add("pool", lambda e: e.nop(), [], [ARENA])

        def load_phaseA_weights(l):
            fence()
            for k in range(8):
                DMA("pool", wF[:, k, :], w_in_d[l][k * 128:(k + 1) * 128, 0:NFC], [ARENA], T_("wF%d" % k), "wF%d" % k)
                DMA("pool", wT[:, k, :], w_in_d[l][k * 128:(k + 1) * 128, NFC:NFC + NTC], [ARENA], T_("wT%d" % k), "wT%d" % k)
            for k in range(8):
                DMA("pool", wO[:, k, :], w_out_d[l][k * 128:(k + 1) * 128, :], [ARENA], T_("wO%d" % k), "wO%d" % k)

        def load_phaseB_weights(l, hf):
            fence()
            for k in range(8):
                DMA("pool", w1h[:, k, :], w_ff1_d[l][k * 128:(k + 1) * 128, hf * 2048:(hf + 1) * 2048], [ARENA], T_("w1h%d" % k), "w1h%d" % k)
            for k in range(16):
                r0 = hf * 2048 + k * 128
                DMA("pool", w2h[:, k, :], w_ff2_d[l][r0:r0 + 128, :], [ARENA], T_("w2h%d" % k), "w2h%d" % k)

        def norm_to_hT(gvec, shbase, gname):
            for bl in range(2):
                ACT(tmpY[:], xt3[:, bl, :], AF.Square, T_("xt"), T_("tmpY", "stat"), accum=stat[:, bl:bl + 1])
            ACT(stat[:, 2:4], stat[:, 0:2], AF.Ln, T_("stat", "cst"), T_("stat"), scale=1.0 / D, bias=C_EPS)
            ACT(stat[:, 4:6], stat[:, 2:4], AF.Exp, T_("stat"), T_("stat"), scale=-0.5)
            for bl in range(2):
                TS("dve", xn3[:, bl, :], xt3[:, bl, :], stat[:, 4 + bl:5 + bl], None, ALU.mult, None, T_("xt", "stat"), T_("xn"))
            for bl in range(2):
                for fc in range(8):
                    TR(ptr3[:, fc, 0:128], xn3[:, bl, fc * 128:(fc + 1) * 128], T_("xn"), T_("ptr"))
                for fc in range(8):
                    ACT(hT3[:, fc, bl * 128:(bl + 1) * 128], ptr3[:, fc, 0:128], AF.Identity, T_("ptr", gname, "modsb"), T_("hT"),
                        scale=gvec[:, fc:fc + 1], bias=modsb[:, shbase + fc:shbase + fc + 1])

        def fproj(name, dst_ps, dst_cols, pst):
            off, w = FCH[name]
            for k in range(8):
                MM(dst_ps[0:w, dst_cols[0]:dst_cols[1]], wF[:, k, off:off + w], hT3[:, k, :], k == 0, k == 7,
                   [ARENA, tk("wF%d" % k), tk("hT")], [pst])

        def core(nch, nh, kd, vw, vcol, Sname, bm, q_ap, q_r, k_ap, k_r, e1_ap, e1_r, e2_ap, e2_r, sc, scn):
            i = rr["ft"] % 2
            rr["ft"] += 1
            qt, kt, kk = qT[i], kT[i], ktok[i]
            qn, kn, kkn = "qT%d" % i, "kT%d" % i, "ktok%d" % i
            TTo("dve", qt[0:nch, :], q_ap, e1_ap, ALU.mult, q_r + e1_r, T_(qn))
            TTo("dve", kt[0:nch, :], k_ap, e2_ap, ALU.mult, k_r + e2_r, T_(kn))
            hmb = bm_gla if kd == 32 else bm_hg
            hmn = "bm_gla" if kd == 32 else "bm_hg"
            for h in range(nh):
                STT(qp[h][0:nch, :], q_ap, hmb[0:nch, h * 64:h * 64 + 1], e1_ap, ALU.mult, ALU.mult, q_r + e1_r + T_(hmn), T_("qp%d" % h))
            for bl in range(2):
                TR(ptr[:, bl * 128:bl * 128 + nch], kt[0:nch, bl * 128:(bl + 1) * 128], T_(kn), T_("ptr"))
            kk3 = kk[:].rearrange("p (b c) -> p b c", b=2)
            CP("act", kk3[:, :, 0:nch], ptr[:, 0:256].rearrange("p (b c) -> p b c", b=2)[:, :, 0:nch], T_("ptr"), T_(kkn))
            Sb = S[Sname]
            Sn = "S_" + Sname
            nv = nh * vw
            for bl in range(2):
                for c in range(2):
                    t0 = bl * 128 + c * 64
                    for h in range(nh):
                        MM(pP[c * 64:(c + 1) * 64, h * 64:(h + 1) * 64], kt[0:nch, t0:t0 + 64],
                           qp[h][0:nch, t0:t0 + 64], True, True, T_(kn, "qp%d" % h), T_("pP"))
                ip = rr["pt"] % 2
                rr["pt"] += 1
                ptb, ptn = PT[ip], "PT%d" % ip
                P.add("dve", lambda e, ptb=ptb: e.copy_predicated(out=ptb[:, 0:nh * 64], mask=maskp[:, 0:nh * 64], data=pP[:, 0:nh * 64]),
                      T_("pP", "maskp"), T_(ptn))
                for c in range(2):
                    ci = bl * 2 + c
                    t0 = bl * 128 + c * 64
                    isr = rr["sr"] % 2
                    rr["sr"] += 1
                    srb, srn = Sr[isr], "Sr%d" % isr
                    STT(srb[0:nch, 0:nv], Sb[0:nch, 0:nv], sc[0:nch, ci:ci + 1], bm[0:nch, 0:nv], ALU.mult, ALU.mult,
                        T_(Sn, scn), T_(srn))
                    MM(pO[c * 64:(c + 1) * 64, 0:nv], qt[0:nch, t0:t0 + 64], srb[0:nch, 0:nv], True, False, T_(qn, srn), T_("pO"))
                    for h in range(nh):
                        MM(pO[c * 64:(c + 1) * 64, h * vw:(h + 1) * vw], ptb[c * 64:(c + 1) * 64, h * 64:(h + 1) * 64],
                           V3[c * 64:(c + 1) * 64, bl, vcol + h * vw:vcol + (h + 1) * vw], False, h == nh - 1, T_(ptn, "Vsb"), T_("pO"))
                    MM(pU[0:nch, 0:nv], kk3[c * 64:(c + 1) * 64, bl, 0:nch], V3[c * 64:(c + 1) * 64, bl, vcol:vcol + nv], True, True,
                       T_(kkn, "Vsb"), T_("pU"))
                    TS("pool", Sb[0:nch, 0:nv], Sb[0:nch, 0:nv], sc[0:nch, 4 + ci:5 + ci], None, ALU.mult, None, T_(Sn, scn), T_(Sn))
                    STT(Sb[0:nch, 0:nv], pU[0:nch, 0:nv], sc[0:nch, 8 + ci:9 + ci], Sb[0:nch, 0:nv], ALU.mult, ALU.add,
                        T_("pU", Sn, scn), T_(Sn))
                CP("act", o3[:, bl, vcol:vcol + nv], pO[:, 0:nv], T_("pO"), T_("o_sb"))

        def chunk_scalars(nch, sA, sc, scn):
            b3 = bF[0:nch, :].rearrange("p (c j) -> p c j", j=64)
            d3 = dF[0:nch, :].rearrange("p (c j) -> p c j", j=64)
            ACT(sc[0:nch, 0:4], b3[:, :, 31], AF.Exp, T_("bF"), T_(scn), scale=sA)
            ACT(sc[0:nch, 4:8], b3[:, :, 63], AF.Exp, T_("bF"), T_(scn), scale=sA)
            ACT(sc[0:nch, 8:12], d3[:, :, 63], AF.Exp, T_("dF"), T_(scn), scale=sA)

        def scan_and_exps(nch, sA, biasq, biask, sc, scn):
            P.add("dve", lambda e: e.tensor_tensor_scan(out=bF[0:nch, :], data0=rmask[0:nch, :], data1=gF[0:nch, :], initial=0.0,
                                                         op0=ALU.mult, op1=ALU.add), T_("rmask", "gF"), T_("bF"))
            b3 = bF[0:nch, :].rearrange("p (c j) -> p c j", j=64)
            d3 = dF[0:nch, :].rearrange("p (c j) -> p c j", j=64)
            TTo("dve", d3, b3, b3[:, :, 31:32].to_broadcast([nch, 4, 64]), ALU.subtract, T_("bF"), T_("dF"))
            ACT(E1[0:nch, :], dF[0:nch, :], AF.Exp, T_("dF", "cst"), T_("E1"), scale=sA, bias=biasq[0:nch, :])
            ACT(E2[0:nch, :], dF[0:nch, :], AF.Exp, T_("dF", "cst"), T_("E2"), scale=-sA, bias=biask[0:nch, :])
            chunk_scalars(nch, sA, sc, scn)

        def phaseA_tile(l, t, src_d, src_tok):
            DMA("sp", xt3, tile_rows(src_d, t), [src_tok], T_("xt"), "xt")
            norm_to_hT(g1, 0, "g1")
            for bl in range(2):
                for cg in range(4):
                    ps, pst = next_mm()
                    for k in range(8):
                        MM(ps[:, 0:512], hT3[:, k, bl * 128:(bl + 1) * 128], wT[:, k, cg * 512:(cg + 1) * 512], k == 0, k == 7,
                           [ARENA, tk("wT%d" % k), tk("hT")], [pst])
                    if cg == 0:
                        CP("act", V3[:, bl, 0:512], ps[:, 0:512], [pst], T_("Vsb"))
                    elif cg == 1:
                        CP("act", V3[:, bl, 512:768], ps[:, 0:256], [pst], T_("Vsb"))
                        CP("act", V3[:, bl, 768:1028].rearrange("p (h c) -> p h c", h=4)[:, :, 0:64],
                           ps[:, 256:512].rearrange("p (h c) -> p h c", h=4), [pst], T_("Vsb"))
                    else:
                        c0 = (cg - 2) * 512
                        ACT(bufA[:, 0:512], ps[:, 0:512], AF.Tanh, [pst], T_("bufA"), scale=0.5)
                        if cg == 2:
                            STT(bufA[:, 0:384], bufA[:, 0:384], 1.0, ps[:, 0:384], ALU.add, ALU.mult, [tk("bufA"), pst], T_("bufA"))
                            TTo("pool", g3[:, bl, 0:384], bufA[:, 0:384], gain_h[:, 0:384], ALU.mult, T_("bufA", "gain_h"), T_("gts"))
                            STT(g3[:, bl, 384:512], bufA[:, 384:512], 1.0, gain_h[:, 384:512], ALU.add, ALU.mult, T_("bufA", "gain_h"), T_("gts"))
                        else:
                            STT(g3[:, bl, 512:1024], bufA[:, 0:512], 1.0, gain_h[:, 512:1024], ALU.add, ALU.mult, T_("bufA", "gain_h"), T_("gts"))
            ps, pst = next_mm()
            fproj("glow", ps, (0, TT_), pst)
            CP("act", glowT[0:16, :], ps[0:16, 0:TT_], [pst], T_("glowT"))
            for g in range(2):
                ps, pst = next_mm()
                fproj("gq%d" % g, ps, (0, TT_), pst)
                fproj("gk%d" % g, ps, (TT_, 2 * TT_), pst)
                MM(pX[0:96, 0:TT_], wgate[0:16, g * 96:(g + 1) * 96], glowT[0:16, :], True, True, T_("wgate", "glowT"), T_("pX"))
                ACT(th[0:96, :], pX[0:96, 0:TT_], AF.Exp, T_("pX", "nbgate"), T_("th"), scale=-1.0, bias=nbgate[0:96, g:g + 1])
                ACT(gF[0:96, :], th[0:96, :], AF.Ln, T_("th", "cst"), T_("gF"), scale=1.0, bias=C_ONE[0:96, :])
                sc = csc[rr["ft"] % 2]; scn = "csc%d" % (rr["ft"] % 2)
                scan_and_exps(96, -1.0 / 16.0, C_LNQ, C_ZERO, sc, scn)
                core(96, 3, 32, 64, g * 192, "g%d" % g, bm_gla, ps[0:96, 0:TT_], [pst], ps[0:96, TT_:2 * TT_], [pst],
                     E1[0:96, :], T_("E1"), E2[0:96, :], T_("E2"), sc, scn)
            for h_ in range(3):
                ps, pst = next_mm()
                fproj("hq%d" % h_, ps, (0, TT_), pst)
                fproj("hf%d" % h_, ps, (TT_, 2 * TT_), pst)
                li = l * 3 + h_
                ACT(th[:, :], ps[:, TT_:2 * TT_], AF.Tanh, [pst], T_("th"), scale=0.5)
                ACT(gF[:, :], th[:, :], AF.Ln, T_("th", "hsc", "hbi"), T_("gF"), scale=hsc[:, li:li + 1], bias=hbi[:, li:li + 1])
                TS("pool", ksrc[:, :], th[:, :], hnsc[:, li:li + 1], hsc[:, li:li + 1], ALU.mult, ALU.add, T_("th", "hnsc", "hsc"), T_("ksrc"))
                ACT(th[:, :], ps[:, 0:TT_], AF.Tanh, [pst, tk("ksrc"), tk("gF")], T_("th"), scale=0.5)
                STT(qsrc[:, :], th[:, :], 1.0, ps[:, 0:TT_], ALU.add, ALU.mult, [tk("th"), pst], T_("qsrc"))
                sc = csc[rr["ft"] % 2]; scn = "csc%d" % (rr["ft"] % 2)
                scan_and_exps(128, 1.0, C_LNH, C_ZERO, sc, scn)
                core(128, 2, 64, 64, 384 + h_ * 128, "h%d" % h_, bm_hg, qsrc[:, :], T_("qsrc"), ksrc[:, :], T_("ksrc"),
                     E1[:, :], T_("E1"), E2[:, :], T_("E2"), sc, scn)
            ps, pst = next_mm()
            fproj("mimf", ps, (0, TT_), pst)
            ACT(x8[0:8, :], ps[0:8, 0:TT_], AF.Identity, [pst, tk("mgb")], T_("x8"), scale=1.0, bias=mgb[0:8, :])
            ACT(l8[0:8, :], ps[0:8, 0:TT_], AF.Exp, [pst, tk("nmgb")], T_("l8"), scale=-1.0, bias=nmgb[0:8, :])
            ACT(l8[0:8, :], l8[0:8, :], AF.Ln, T_("l8", "cst"), T_("l8"), scale=1.0, bias=C_ONE[0:8, :])
            P.add("dve", lambda e: e.tensor_tensor_scan(out=b8[0:8, :], data0=rmask[0:8, :], data1=l8[0:8, :], initial=0.0,
                                                         op0=ALU.mult, op1=ALU.add), T_("rmask", "l8"), T_("b8"))
            b83 = b8[0:8, :].rearrange("p (c j) -> p c j", j=64)
            d83 = d8[0:8, :].rearrange("p (c j) -> p c j", j=64)
            TTo("dve", d83, b83, b83[:, :, 31:32].to_broadcast([8, 4, 64]), ALU.subtract, T_("b8"), T_("d8"))
            CP("dve", cs8[0:8, 0:4], b83[:, :, 31], T_("b8"), T_("cs8"))
            CP("dve", cs8[0:8, 4:8], b83[:, :, 63], T_("b8"), T_("cs8"))
            CP("dve", cs8[0:8, 8:12], d83[:, :, 63], T_("d8"), T_("cs8"))
            sel3 = sel[0:8, :].rearrange("r (s m) -> r s m", s=6)
            for m_ in range(2):
                srcs = []
                for qi, nm in ((0, "mq%d" % m_), (1, "mk%d" % m_)):
                    ps, pst = next_mm()
                    fproj(nm, ps, (0, TT_), pst)
                    ci_ = qi * 2 + m_
                    cb, cbn = convb[ci_], "convb%d" % ci_
                    CP("act", cb[:, 3:3 + TT_], ps[:, 0:TT_], [pst], T_(cbn))
                    dst = qsrc if qi == 0 else ksrc
                    dn = "qsrc" if qi == 0 else "ksrc"
                    TS("dve", dst[:, :], cb[:, 0:TT_], convw[:, ci_ * 4:ci_ * 4 + 1], None, ALU.mult, None, T_(cbn, "convw"), T_(dn))
                    for k_ in range(1, 4):
                        STT(dst[:, :], cb[:, k_:k_ + TT_], convw[:, ci_ * 4 + k_:ci_ * 4 + k_ + 1], dst[:, :], ALU.mult, ALU.add,
                            T_(cbn, "convw", dn), T_(dn))
                    CP("pool", cb[:, 0:3], cb[:, TT_:TT_ + 3], T_(cbn), T_(cbn))
                    ACT(th[:, :], dst[:, :], AF.Tanh, T_(dn), T_("th"), scale=0.5)
                    STT(dst[:, :], th[:, :], 1.0, dst[:, :], ALU.add, ALU.mult, T_("th", dn), T_(dn))
                MM(pX[:, 0:TT_], sel3[:, m_, :], d8[0:8, :], True, True, T_("sel", "d8"), T_("pX"))
                ACT(E1[:, :], pX[:, 0:TT_], AF.Exp, T_("pX", "cst"), T_("E1"), scale=1.0, bias=C_LNH)
                MM(pX[:, 0:TT_], sel3[:, 2 + m_, :], d8[0:8, :], True, False, T_("sel", "d8"), T_("pX"))
                MM(pX[:, 0:TT_], sel3[:, 4 + m_, :], x8[0:8, :], False, True, T_("sel", "x8"), T_("pX"))
                ACT(E2[:, :], pX[:, 0:TT_], AF.Exp, T_("pX", "cst"), T_("E2"), scale=1.0, bias=C_LNK)
                sc = csc[rr["ft"] % 2]; scn = "csc%d" % (rr["ft"] % 2)
                MM(pX[:, 0:12], sel3[:, m_, :], cs8[0:8, 0:12], True, True, T_("sel", "cs8"), T_("pX"))
                ACT(sc[:, 0:12], pX[:, 0:12], AF.Exp, T_("pX"), T_(scn))
                core(128, 2, 64, 65, 768 + m_ * 130, "m%d" % m_, bm_ml, qsrc[:, :], T_("qsrc"), ksrc[:, :], T_("ksrc"),
                     E1[:, :], T_("E1"), E2[:, :], T_("E2"), sc, scn)
            for bl in range(2):
                ogh = o3[:, bl, 0:768]
                ACT(bufA[:, 0:768], ogh, AF.Square, T_("o_sb"), T_("bufA"))
                RED(stat[:, 8:20], bufA[:, 0:768].rearrange("p (h c) -> p h c", c=64), T_("bufA"), T_("stat"))
                oml = o3[:, bl, 768:1028].rearrange("p (h c) -> p h c", c=65)
                ACT(stat[:, 32:36], oml[:, :, 64], AF.Abs, T_("o_sb"), T_("stat"))
                TS("dve", stat[:, 32:36], stat[:, 32:36], 1.0, None, ALU.max, None, T_("stat"), T_("stat"))
                P.add("dve", lambda e: e.reciprocal(out=stat[:, 36:40], in_=stat[:, 32:36]), T_("stat"), T_("stat"))
                hml = bufB[:, 0:256].rearrange("p (h c) -> p h c", c=64)
                TTo("dve", hml, oml[:, :, 0:64], stat[:, 36:40].unsqueeze(2).to_broadcast([128, 4, 64]), ALU.mult, T_("o_sb", "stat"), T_("bufB"))
                RED(stat[:, 40:44], hml, T_("bufB"), T_("stat"))
                TS("dve", stat[:, 40:44], stat[:, 40:44], -1.0 / 64.0, None, ALU.mult, None, T_("stat"), T_("stat"))
                TTo("dve", hml, hml, stat[:, 40:44].unsqueeze(2).to_broadcast([128, 4, 64]), ALU.add, T_("bufB", "stat"), T_("bufB"))
                ACT(bufA[:, 768:1024], bufB[:, 0:256], AF.Square, T_("bufB"), T_("bufA"))
                RED(stat[:, 20:24], bufA[:, 768:1024].rearrange("p (h c) -> p h c", c=64), T_("bufA"), T_("stat"))
                ACT(stat[:, 8:24], stat[:, 8:24], AF.Ln, T_("stat", "cst"), T_("stat"), scale=1.0 / 64.0, bias=C_EPS)
                ACT(stat[:, 8:24], stat[:, 8:24], AF.Exp, T_("stat"), T_("stat"), scale=-0.5)
                TTo("dve", bufA[:, 0:768].rearrange("p (h c) -> p h c", c=64), ogh.rearrange("p (h c) -> p h c", c=64),
                    stat[:, 8:20].unsqueeze(2).to_broadcast([128, 12, 64]), ALU.mult, T_("o_sb", "stat"), T_("bufA"))
                TTo("dve", bufA[:, 768:1024].rearrange("p (h c) -> p h c", c=64), hml,
                    stat[:, 20:24].unsqueeze(2).to_broadcast([128, 4, 64]), ALU.mult, T_("bufB", "stat"), T_("bufA"))
                TTo("pool", xn3[:, bl, :], bufA[:, :], g3[:, bl, :], ALU.mult, T_("bufA", "gts"), T_("xn"))
            for bl in range(2):
                for fc in range(8):
                    TR(ptr3[:, fc, 0:128], xn3[:, bl, fc * 128:(fc + 1) * 128], T_("xn"), T_("ptr"))
                CP("act", hT3[:, :, bl * 128:(bl + 1) * 128], ptr3[:, :, 0:128], T_("ptr"), T_("hT"))
            for bl in range(2):
                for cg in range(2):
                    ps, pst = next_mm()
                    for k in range(8):
                        MM(ps[:, 0:512], hT3[:, k, bl * 128:(bl + 1) * 128], wO[:, k, cg * 512:(cg + 1) * 512], k == 0, k == 7,
                           [ARENA, tk("wO%d" % k), tk("hT")], [pst])
                    TTo("dve", tmpY[:, cg * 512:(cg + 1) * 512], ps[:, 0:512], gate1_bc[:, cg * 512:(cg + 1) * 512], ALU.mult,
                        [pst, tk("gate1_bc")], T_("tmpY"))
                    TTo("pool", xt3[:, bl, cg * 512:(cg + 1) * 512], xt3[:, bl, cg * 512:(cg + 1) * 512], tmpY[:, cg * 512:(cg + 1) * 512],
                        ALU.add, T_("xt", "tmpY"), T_("xt"))
            DMA("sp", tile_rows(xb_d, t), xt3, T_("xt"), [xtok("xb", t)], "xst")

        def phaseB_tile(l, t, hf, last):
            DMA("sp", xt3, tile_rows(xb_d, t), [xtok("xb", t)], T_("xt"), "xt")
            if hf == 1:
                DMA("sp", xacc3, tile_rows(xa_d, t), [xtok("xa", t)], T_("xacc"), "xacc")
            norm_to_hT(g2, 24, "g2")
            aT_A = bufA[:].bitcast(BF16).rearrange("p (k t) -> p k t", k=8)
            aT_B = bufB[:].bitcast(BF16).rearrange("p (k t) -> p k t", k=8)
            for pr in range(8):
                ps, pst = next_mm()
                for j in range(2):
                    ffc = pr * 2 + j
                    for k in range(8):
                        MM(ps[:, j * TT_:(j + 1) * TT_], w1h[:, k, ffc * 128:(ffc + 1) * 128], hT3[:, k, :], k == 0, k == 7,
                           [ARENA, tk("w1h%d" % k), tk("hT")], [pst])
                ACT(tmpY[:, 0:512], ps[:, 0:512], AF.Relu, [pst], T_("tmpY"))
                if pr < 4:
                    dstv, dn = aT_A[:, pr * 2:pr * 2 + 2, :], "bufA"
                else:
                    dstv, dn = aT_B[:, (pr - 4) * 2:(pr - 4) * 2 + 2, :], "bufB"
                TTo("pool", dstv, tmpY[:, 0:512].rearrange("p (k t) -> p k t", k=2), tmpY[:, 0:512].rearrange("p (k t) -> p k t", k=2),
                    ALU.mult, T_("tmpY"), T_(dn))
            base = xt3 if hf == 0 else xacc3
            bn = "xt" if hf == 0 else "xacc"
            for bl in range(2):
                for cg in range(2):
                    ps, pst = next_mm()
                    for ffc in range(16):
                        av, an = (aT_A, "bufA") if ffc < 8 else (aT_B, "bufB")
                        MM(ps[:, 0:512], av[:, ffc % 8, bl * 128:(bl + 1) * 128], w2h[:, ffc, cg * 512:(cg + 1) * 512], ffc == 0, ffc == 15,
                           [ARENA, tk("w2h%d" % ffc), tk(an)], [pst])
                    TTo("dve", tmpY[:, cg * 512:(cg + 1) * 512], ps[:, 0:512], gate2_bc[:, cg * 512:(cg + 1) * 512], ALU.mult,
                        [pst, tk("gate2_bc")], T_("tmpY"))
                    TTo("pool", base[:, bl, cg * 512:(cg + 1) * 512], base[:, bl, cg * 512:(cg + 1) * 512], tmpY[:, cg * 512:(cg + 1) * 512],
                        ALU.add, T_(bn, "tmpY"), T_(bn))
            if not last:
                DMA("sp", tile_rows(xa_d, t), base, T_(bn), [xtok("xa", t)], "xst")
            else:
                for bl in range(2):
                    ACT(tmpY[:], base[:, bl, :], AF.Square, T_(bn), T_("tmpY", "stat"), accum=stat[:, bl:bl + 1])
                ACT(stat[:, 2:4], stat[:, 0:2], AF.Ln, T_("stat", "cst"), T_("stat"), scale=1.0 / D, bias=C_EPS)
                ACT(stat[:, 4:6], stat[:, 2:4], AF.Exp, T_("stat"), T_("stat"), scale=-0.5)
                for bl in range(2):
                    STT(base[:, bl, :], base[:, bl, :], stat[:, 4 + bl:5 + bl], gain_h[:, :], ALU.mult, ALU.mult, T_(bn, "stat", "gain_h"), T_(bn))
                DMA("sp", tile_rows(out_d, t), base, T_(bn), [xtok("out", t)], "xst")

        import os
        KSTOP = os.environ.get("KSTOP", "")
        for l in range(nlayers):
            if KSTOP == "setup":
                break
            ada_ln(l)
            if KSTOP == "ada":
                break
            layer_small(l)
            if KSTOP == "small":
                break
            load_phaseA_weights(l)
            if KSTOP == "w":
                break
            for t in range(NT):
                if KSTOP.startswith("A") and t >= int(KSTOP[1:]):
                    break
                if l == 0:
                    phaseA_tile(l, t, x_d, tk("x_in"))
                else:
                    phaseA_tile(l, t, xa_d, xtok("xa", t))
            if KSTOP.startswith("A"):
                break
            lastl = (l == nlayers - 1)
            for hf in range(2):
                load_phaseB_weights(l, hf)
                if lastl and hf == 1:
                    DMA("sp", gain_h[:], fnorm_d[0:1, :].partition_broadcast(128), [], T_("gain_h"), "c_gain")
                for t in range(NT):
                    phaseB_tile(l, t, hf, lastl and hf == 1)
        fin = P.add("sp", lambda e: e.nop(), [xtok("out", t) for t in range(NT)], [])
        P.finalize_and_emit()
    return nc


def kernel(**inputs):
    com = host_prep(inputs)
    x = np.asarray(inputs["x"], np.float32)
    c = np.asarray(inputs["c"], np.float32)
    nb = x.shape[0]
    nc = build_program()
    in_maps = []
    for b in range(nb):
        m = dict(com)
        m["x"] = np.ascontiguousarray(x[b])
        m["cT"] = np.ascontiguousarray(c[b].reshape(8, 128).T)
        in_maps.append(m)
    res = run_bass_kernel_spmd(nc, in_maps, core_ids=list(range(nb)))
    out = np.stack([np.asarray(res.results[b]["out"], np.float32) for b in range(nb)], axis=0)
    return out
```

```python
from concourse.bass_utils import run_bass_kernel_spmd

import numpy as np
import concourse.bass as bass
import concourse.mybir as mybir

F32 = mybir.dt.float32
BF16 = mybir.dt.bfloat16
I32 = mybir.dt.int32
AF = mybir.ActivationFunctionType
from concourse.alu_op_type import AluOpType as ALU
AX = mybir.AxisListType


class Tok:
    __slots__ = ("name", "last_w", "readers", "excl")

    def __init__(self, name, excl=False):
        self.name = name
        self.last_w = None
        self.readers = []
        self.excl = excl


class Op:
    __slots__ = ("eng", "fn", "deps", "sig", "cnt", "dma_key", "waits", "gidx")

    def __init__(self, eng, fn, dma_key=None):
        self.eng = eng
        self.fn = fn
        self.deps = []
        self.sig = False
        self.cnt = 0
        self.dma_key = dma_key
        self.waits = []


ENGS = ("pe", "act", "dve", "pool", "sp")


class Prog:
    def __init__(self, nc):
        self.nc = nc
        self.ops = {e: [] for e in ENGS}
        self.n = 0
        self.dma_keys = []

    def add(self, eng, fn, reads=(), writes=(), dma_key=None):
        op = Op(eng, fn, dma_key)
        op.gidx = self.n
        self.n += 1
        if dma_key is not None and dma_key not in self.dma_keys:
            self.dma_keys.append(dma_key)
        deps = []
        for t in reads:
            if t.excl:
                if t.last_w is not None:
                    deps.append(t.last_w)
                deps.extend(t.readers)
                t.last_w = op
                t.readers = []
            else:
                if t.last_w is not None:
                    deps.append(t.last_w)
                if dma_key is None:
                    t.readers = [r for r in t.readers if not (r.eng == eng and r.dma_key is None)]
                t.readers.append(op)
        for t in writes:
            if t.last_w is not None:
                deps.append(t.last_w)
            deps.extend(r for r in t.readers if r is not op)
            t.last_w = op
            t.readers = []
        seen = set()
        for d in deps:
            if d is op or id(d) in seen:
                continue
            seen.add(id(d))
            if d.eng == "pe" and eng == "pe" and d.dma_key is None and dma_key is None:
                continue
            op.deps.append(d)
        self.ops[eng].append(op)
        return op

    def finalize_and_emit(self):
        nc = self.nc
        for e in ENGS:
            for op in self.ops[e]:
                if op.dma_key is not None:
                    op.sig = True
                for d in op.deps:
                    d.sig = True
        dma_cnt = {k: 0 for k in self.dma_keys}
        for e in ENGS:
            c = 0
            for op in self.ops[e]:
                if op.dma_key is not None:
                    if op.sig:
                        dma_cnt[op.dma_key] += 16
                        op.cnt = dma_cnt[op.dma_key]
                elif op.sig:
                    c += 1
                    op.cnt = c
        for e in ENGS:
            waited = {}
            for op in self.ops[e]:
                need = {}
                for d in op.deps:
                    key = ("dma", d.dma_key) if d.dma_key is not None else ("eng", d.eng)
                    if d.cnt > need.get(key, 0):
                        need[key] = d.cnt
                for key, v in need.items():
                    if waited.get(key, 0) >= v:
                        continue
                    waited[key] = v
                    op.waits.append((key, v))
        import contextlib
        with contextlib.ExitStack() as es:
            sems = {}
            for e in ENGS:
                sems[("eng", e)] = es.enter_context(nc.semaphore("s_" + e))
            for k in self.dma_keys:
                sems[("dma", k)] = es.enter_context(nc.semaphore("d_" + str(k)))
            block = es.enter_context(nc.Block())
            ops = self.ops

            def run(eng_obj, e):
                for op in ops[e]:
                    for key, v in op.waits:
                        eng_obj.wait_ge(sems[key], v)
                    ins = op.fn(eng_obj)
                    if op.sig:
                        if op.dma_key is not None:
                            ins.then_inc(sems[("dma", op.dma_key)], 16)
                        else:
                            ins.then_inc(sems[("eng", e)], 1)

            @block.tensor
            def _(eng):
                run(eng, "pe")

            @block.scalar
            def _(eng):
                run(eng, "act")

            @block.vector
            def _(eng):
                run(eng, "dve")

            @block.gpsimd
            def _(eng):
                run(eng, "pool")

            @block.sync
            def _(eng):
                run(eng, "sp")

D = 1024
SEQ = 4096
NL = 4
DFF = 4096
TT_ = 256
NT = SEQ // TT_
EPS = 1e-6
NFC = 1688
NTC = 2048
LN_HALF = float(np.log(0.5))

FCH = {}
_o = 0
for _n, _w in [("gq0", 96), ("gq1", 96), ("gk0", 96), ("gk1", 96), ("glow", 16),
               ("hq0", 128), ("hq1", 128), ("hq2", 128), ("hf0", 128), ("hf1", 128), ("hf2", 128),
               ("mq0", 128), ("mq1", 128), ("mk0", 128), ("mk1", 128), ("mimf", 8)]:
    FCH[_n] = (_o, _w)
    _o += _w
assert _o == NFC


def host_prep(inputs):
    f32 = np.float32
    w_in = np.asarray(inputs["w_in"], f32)
    gq, gk, gv, glow, gout = (0, 192), (192, 384), (384, 768), (768, 784), (784, 1168)
    hq, hf, hi, hout = (1168, 1552), (1552, 1936), (1936, 2320), (2320, 2704)
    mqk, mv, mi, mf, mout = (2704, 3216), (3216, 3472), (3472, 3476), (3476, 3480), (3480, 3736)
    cols = []
    cols += list(range(gq[0], gq[1])) + list(range(gk[0], gk[1])) + list(range(glow[0], glow[1]))
    cols += list(range(hq[0], hq[1])) + list(range(hf[0], hf[1]))
    cols += list(range(mqk[0], mqk[1])) + list(range(mi[0], mi[1])) + list(range(mf[0], mf[1]))
    assert len(cols) == NFC
    cols += list(range(gv[0], gv[1])) + list(range(hi[0], hi[1])) + list(range(mv[0], mv[1]))
    cols += list(range(gout[0], gout[1])) + list(range(hout[0], hout[1])) + list(range(mout[0], mout[1]))
    assert len(cols) == NFC + NTC
    w_in_re = np.ascontiguousarray(w_in[:, :, cols])

    def fmaj(v, nchunk):
        v = np.asarray(v, f32)
        return np.ascontiguousarray(np.swapaxes(v.reshape(v.shape[:-1] + (nchunk, 128)), -1, -2))

    com = {}
    com["w_in"] = w_in_re
    com["w_ada"] = np.asarray(inputs["w_ada"], f32)
    com["w_out"] = np.asarray(inputs["w_out"], f32)
    com["w_ff1"] = np.asarray(inputs["w_ff1"], f32)
    com["w_ff2"] = np.asarray(inputs["w_ff2"], f32)
    com["b_ada"] = fmaj(inputs["b_ada"], 48)
    com["nmix"] = fmaj(inputs["norm_mix"], 8)
    com["nmlp"] = fmaj(inputs["norm_mlp"], 8)
    com["wgate"] = np.asarray(inputs["gla_w_gate"], f32)
    bg = np.asarray(inputs["gla_b_gate"], f32)
    com["bgate"] = np.ascontiguousarray(np.swapaxes(bg.reshape(NL, 2, 96), 1, 2))
    lbl = np.asarray(inputs["hgrn_lb_logits"], f32)
    com["lblog"] = np.ascontiguousarray(np.transpose(lbl.reshape(NL, 3, 128), (2, 0, 1)))
    gn = np.asarray(inputs["gla_norm"], f32)
    hn = np.asarray(inputs["hgrn_norm"], f32)
    mn = np.asarray(inputs["mlstm_norm"], f32)
    com["gainrow"] = np.ascontiguousarray(np.concatenate([np.tile(gn, (1, 6)), np.tile(hn, (1, 6)), mn], axis=1))
    cw = np.asarray(inputs["mlstm_conv"], f32)
    com["convw"] = np.ascontiguousarray(np.transpose(cw.reshape(NL, 4, 4, 128), (0, 3, 2, 1)))
    com["mgb"] = np.ascontiguousarray(np.asarray(inputs["mlstm_gate_b"], f32).reshape(NL, 8, 1))
    com["fnorm"] = np.ascontiguousarray(np.asarray(inputs["final_norm"], f32).reshape(1, D))
    com["ident"] = np.eye(128, dtype=f32)
    p = np.arange(128)[:, None]
    c = np.arange(192)[None, :]
    com["maskp"] = ((p % 64) <= (c % 64)).astype(f32)
    rm = np.ones((128, TT_), f32)
    rm[:, ::64] = 0.0
    com["rmask"] = rm
    c192 = np.arange(192)[None, :]
    com["bm_gla"] = ((p // 32) == (c192 // 64)).astype(f32)[:, :192]
    c128 = np.arange(128)[None, :]
    com["bm_hg"] = ((p // 64) == (c128 // 64)).astype(f32)
    c130 = np.arange(130)[None, :]
    com["bm_ml"] = ((p // 64) == (c130 // 65)).astype(f32)
    sel = np.zeros((8, 6, 128), f32)
    for t in range(2):
        for m in range(128):
            sel[4 + 2 * t + m // 64, t, m] = -1.0
            sel[4 + 2 * t + m // 64, 2 + t, m] = 1.0
            sel[2 * t + m // 64, 4 + t, m] = 1.0
    com["sel"] = sel
    return com


def build_program(nlayers=NL, debug_out=None):
    nc = bass.Bass("TRN2", target_bir_lowering=False)
    dr = {}

    def din(name, shape):
        dr[name] = nc.dram_tensor(name, list(shape), F32, kind="ExternalInput").ap()
        return dr[name]

    x_d = din("x", [SEQ, D])
    cT_d = din("cT", [128, 8])
    w_in_d = din("w_in", [NL, D, NFC + NTC])
    w_ada_d = din("w_ada", [NL, D, 6 * D])
    w_out_d = din("w_out", [NL, D, D])
    w_ff1_d = din("w_ff1", [NL, D, DFF])
    w_ff2_d = din("w_ff2", [NL, DFF, D])
    b_ada_d = din("b_ada", [NL, 128, 48])
    nmix_d = din("nmix", [NL, 128, 8])
    nmlp_d = din("nmlp", [NL, 128, 8])
    wgate_d = din("wgate", [NL, 16, 192])
    bgate_d = din("bgate", [NL, 96, 2])
    lblog_d = din("lblog", [128, NL, 3])
    gainrow_d = din("gainrow", [NL, D])
    convw_d = din("convw", [NL, 128, 4, 4])
    mgb_d = din("mgb", [NL, 8, 1])
    fnorm_d = din("fnorm", [1, D])
    ident_d = din("ident", [128, 128])
    maskp_d = din("maskp", [128, 192])
    rmask_d = din("rmask", [128, TT_])
    bm_gla_d = din("bm_gla", [128, 192])
    bm_hg_d = din("bm_hg", [128, 128])
    bm_ml_d = din("bm_ml", [128, 130])
    sel_d = din("sel", [8, 6, 128])
    out_d = nc.dram_tensor("out", [SEQ, D], F32, kind="ExternalOutput").ap()
    xa_d = nc.dram_tensor("xa_scr", [SEQ, D], F32).ap()
    xb_d = nc.dram_tensor("xb_scr", [SEQ, D], F32).ap()

    import contextlib
    es = contextlib.ExitStack()
    with es:
        P = Prog(nc)
        toks = {}

        def tk(name, excl=False):
            if name not in toks:
                toks[name] = Tok(name, excl)
            return toks[name]

        def sb(name, cols, dt=F32, parts=128):
            t = es.enter_context(nc.sbuf_tensor("s_" + name, [parts, cols], dt))
            tk(name)
            return t

        pmm = [es.enter_context(nc.psum_tensor("pmm%d" % i, [128, 512], F32)) for i in range(2)]
        ptr = es.enter_context(nc.psum_tensor("ptr", [128, 1024], BF16))
        pP = es.enter_context(nc.psum_tensor("pP", [128, 512], F32))
        pO = es.enter_context(nc.psum_tensor("pO", [128, 512], F32))
        pUs = [es.enter_context(nc.psum_tensor("pU%d" % i, [128, 512], F32)) for i in range(2)]
        pX = es.enter_context(nc.psum_tensor("pX", [128, 512], F32))
        for n in ["pmm0", "pmm1", "ptr", "pP", "pO", "pU0", "pU1", "pX"]:
            tk(n, excl=True)
        mm_rr = [0]

        def next_mm():
            i = mm_rr[0] % 2
            mm_rr[0] += 1
            return pmm[i], tk("pmm%d" % i)

        WAR_COLS = 38080
        warena = sb("warena", WAR_COLS, BF16)
        wF = warena[:, 0:8 * NFC].rearrange("p (k c) -> p k c", k=8)
        wT = warena[:, 8 * NFC:8 * NFC + 8 * NTC].rearrange("p (k c) -> p k c", k=8)
        wO = warena[:, 8 * (NFC + NTC):8 * (NFC + NTC) + 8192].rearrange("p (k c) -> p k c", k=8)
        w1h = warena[:, 0:16384].rearrange("p (k c) -> p k c", k=8)
        w2h = warena[:, 16384:32768].rearrange("p (k c) -> p k c", k=16)
        ARENA = tk("ARENA")

        ident_f = sb("ident_f", 128)
        ident_b = sb("ident_b", 128, BF16)
        ones_f = sb("ones_f", 128)
        maskp_f = sb("maskp_f", 192)
        maskp = sb("maskp", 192, mybir.dt.uint8)
        rmask = sb("rmask", TT_)
        bm_gla = sb("bm_gla", 192)
        bm_hg = sb("bm_hg", 128)
        bm_ml = sb("bm_ml", 130)
        sel = sb("sel", 6 * 128)
        cst = sb("cst", 8)
        cond = sb("cond", 8)
        ctmp = sb("ctmp", 8)
        lbw = sb("lbw", 3 * NL * 4)
        hsc = sb("hsc", NL * 3)
        hbi = sb("hbi", NL * 3)
        hnsc = sb("hnsc", NL * 3)
        wst = [sb("wst%d" % i, 8 * 256) for i in range(2)]
        modsb = sb("modsb", 48)
        bada = sb("bada", 48)
        nmix = sb("nmix", 8)
        nmlp = sb("nmlp", 8)
        g1 = sb("g1", 8)
        g2 = sb("g2", 8)
        gate1_bc = sb("gate1_bc", D)
        gate2_bc = sb("gate2_bc", D)
        gain_h = sb("gain_h", D)
        diag = [sb("diag%d" % i, 128) for i in range(2)]
        wgate_f = sb("wgate_f", 192)
        wgate = sb("wgate", 192, BF16)
        nbgate = sb("nbgate", 2)
        convw = sb("convw", 16)
        mgb = sb("mgb", 1)
        nmgb = sb("nmgb", 1)

        xt = sb("xt", 2 * D)
        xacc = sb("xacc", 2 * D)
        xn = sb("xn", 2 * D, BF16)
        hT = sb("hT", 8 * TT_, BF16)
        Vsb = sb("Vsb", 2 * 1028, BF16)
        gts = sb("gts", 2 * D, BF16)
        o_sb = sb("o_sb", 2 * 1040)
        bufA = sb("bufA", D)
        bufB = sb("bufB", D)
        tmpY = sb("tmpY", D)
        stat = sb("stat", 64)
        gF = sb("gF", TT_)
        bF = sb("bF", TT_)
        dF = sb("dF", TT_)
        E1 = sb("E1", TT_)
        E2 = sb("E2", TT_)
        th = sb("th", TT_)
        ksrc = sb("ksrc", TT_)
        qsrc = sb("qsrc", TT_)
        glowT = sb("glowT", TT_, BF16)
        qT = [sb("qT%d" % i, TT_, BF16) for i in range(2)]
        kT = [sb("kT%d" % i, TT_, BF16) for i in range(2)]
        ktok = [sb("ktok%d" % i, 2 * 128, BF16) for i in range(2)]
        qp = [sb("qp%d" % i, TT_, BF16) for i in range(3)]
        csc = [sb("csc%d" % i, 12) for i in range(2)]
        cs8 = sb("cs8", 12)
        x8 = sb("x8", TT_)
        l8 = sb("l8", TT_)
        b8 = sb("b8", TT_)
        d8 = sb("d8", TT_)
        convb = [sb("convb%d" % i, TT_ + 3) for i in range(4)]
        S = {}
        for n_, w_ in [("g0", 192), ("g1", 192), ("h0", 128), ("h1", 128), ("h2", 128), ("m0", 130), ("m1", 130)]:
            S[n_] = sb("S_" + n_, w_)
        Sr = [sb("Sr%d" % i, 192, BF16) for i in range(4)]
        ebp = sb("ebp", 8)
        EBP = {"g0": 0, "g1": 1, "h0": 2, "h1": 3, "h2": 4, "m0": 5, "m1": 6}
        PT = [sb("PT%d" % i, 192, BF16) for i in range(2)]

        rr = {"ft": 0, "sr": 0, "pt": 0, "wst": 0, "diag": 0, "pu": 0}

        def T_(*names):
            return [tk(n) for n in names]

        def DMA(eng, out, in_, r, w, key):
            return P.add(eng, lambda e: e.dma_start(out=out, in_=in_), r, w, dma_key=key)

        def MM(out, lhsT, rhs, start, stop, r, w):
            return P.add("pe", lambda e: e.matmul(out, lhsT=lhsT, rhs=rhs, start=start, stop=stop), r, w)

        def TR(out, in_, r, w):
            return P.add("pe", lambda e: e.transpose(out=out, in_=in_, identity=ident_b[0:in_.shape[0], 0:in_.shape[0]]), r + [tk("ident_b")], w)

        def ACT(out, in_, func, r, w, scale=1.0, bias=0.0, accum=None):
            if accum is None:
                return P.add("act", lambda e: e.activation(out=out, in_=in_, func=func, scale=scale, bias=bias), r, w)
            return P.add("act", lambda e: e.activation(out=out, in_=in_, func=func, scale=scale, bias=bias, accum_out=accum), r, w)

        def TTo(eng, out, in0, in1, op, r, w):
            return P.add(eng, lambda e: e.tensor_tensor(out=out, in0=in0, in1=in1, op=op), r, w)

        def TS(eng, out, in0, s1, s2, op0, op1, r, w):
            if op1 is None:
                return P.add(eng, lambda e: e.tensor_scalar(out=out, in0=in0, scalar1=s1, scalar2=None, op0=op0), r, w)
            return P.add(eng, lambda e: e.tensor_scalar(out=out, in0=in0, scalar1=s1, scalar2=s2, op0=op0, op1=op1), r, w)

        def STT(out, in0, scalar, in1, op0, op1, r, w):
            return P.add("dve", lambda e: e.scalar_tensor_tensor(out=out, in0=in0, scalar=scalar, in1=in1, op0=op0, op1=op1), r, w)

        def CP(eng, out, in_, r, w):
            if eng == "act":
                return P.add("act", lambda e: e.copy(out=out, in_=in_), r, w)
            return P.add(eng, lambda e: e.tensor_copy(out=out, in_=in_), r, w)

        def MS(eng, ap, val, w):
            return P.add(eng, lambda e: e.memset(ap, val), [], w)

        def RED(out, in_, r, w):
            return P.add("dve", lambda e: e.tensor_reduce(out=out, in_=in_, axis=AX.X, op=ALU.add), r, w)

        def ld(dst, src, name):
            DMA("sp", dst, src, [], T_(name), "c_" + name)

        ld(ident_f[:], ident_d[:], "ident_f")
        ld(maskp_f[:], maskp_d[:], "maskp_f")
        ld(rmask[:], rmask_d[:], "rmask")
        ld(bm_gla[:], bm_gla_d[:], "bm_gla")
        ld(bm_hg[:], bm_hg_d[:], "bm_hg")
        ld(bm_ml[:], bm_ml_d[:], "bm_ml")
        ld(sel[0:8, :], sel_d[:].rearrange("r s m -> r (s m)"), "sel")
        ld(ctmp[:], cT_d[:], "ctmp")
        ld(lbw[:, 0:NL * 3], lblog_d[:].rearrange("p l t -> p (l t)"), "lbw")
        CP("dve", ident_b[:], ident_f[:], T_("ident_f"), T_("ident_b"))
        CP("dve", maskp[:], maskp_f[:], T_("maskp_f"), T_("maskp"))
        MS("pool", ones_f[:], 1.0, T_("ones_f"))
        MS("pool", cst[:, 0:1], 1.0, T_("cst"))
        MS("pool", cst[:, 1:2], EPS, T_("cst"))
        MS("pool", cst[:, 2:3], LN_HALF, T_("cst"))
        MS("pool", cst[:, 3:4], LN_HALF + float(np.log(0.125)), T_("cst"))
        MS("pool", cst[:, 4:5], float(np.log(32.0 ** -0.5)), T_("cst"))
        MS("pool", cst[:, 5:6], 0.0, T_("cst"))
        for i in range(2):
            MS("pool", PT[i][:], 0.0, T_("PT%d" % i))
            MS("pool", Vsb[:], 1.0, T_("Vsb"))
        C_ONE, C_EPS, C_LNH, C_LNK, C_LNQ, C_ZERO = [cst[:, i:i + 1] for i in range(6)]
        ACT(cond[:], ctmp[:], AF.Tanh, T_("ctmp"), T_("cond"), scale=0.5)
        STT(cond[:], cond[:], 1.0, ctmp[:], ALU.add, ALU.mult, T_("cond", "ctmp"), T_("cond"))
        TS("dve", cond[:], cond[:], 0.5, None, ALU.mult, None, T_("cond"), T_("cond"))
        lg = lbw[:, 0:12].rearrange("p (l t) -> p l t", l=NL)
        mx = lbw[:, 12:15]
        ex = lbw[:, 16:28].rearrange("p (l t) -> p l t", l=NL)
        sm = lbw[:, 28:31]
        lbv = lbw[:, 32:44].rearrange("p (l t) -> p l t", l=NL)
        TTo("dve", mx, lg[:, 0, :], lg[:, 1, :], ALU.max, T_("lbw"), T_("lbw"))
        TTo("dve", mx, mx, lg[:, 2, :], ALU.max, T_("lbw"), T_("lbw"))
        TTo("dve", mx, mx, lg[:, 3, :], ALU.max, T_("lbw"), T_("lbw"))
        for l in range(NL):
            TTo("dve", ex[:, l, :], lg[:, l, :], mx, ALU.subtract, T_("lbw"), T_("lbw"))
        ACT(lbw[:, 16:28], lbw[:, 16:28], AF.Exp, T_("lbw"), T_("lbw"))
        TTo("dve", sm, ex[:, 0, :], ex[:, 1, :], ALU.add, T_("lbw"), T_("lbw"))
        TTo("dve", sm, sm, ex[:, 2, :], ALU.add, T_("lbw"), T_("lbw"))
        TTo("dve", sm, sm, ex[:, 3, :], ALU.add, T_("lbw"), T_("lbw"))
        P.add("dve", lambda e: e.reciprocal(out=sm, in_=sm), T_("lbw"), T_("lbw"))
        for l in range(NL):
            TTo("dve", ex[:, l, :], ex[:, l, :], sm, ALU.mult, T_("lbw"), T_("lbw"))
        MS("dve", lbv[:, 0, :], 0.0, T_("lbw"))
        for l in range(1, NL):
            TTo("dve", lbv[:, l, :], lbv[:, l - 1, :], ex[:, l, :], ALU.add, T_("lbw"), T_("lbw"))
        TS("dve", hsc[:], lbw[:, 32:44], -0.5, 0.5, ALU.mult, ALU.add, T_("lbw"), T_("hsc"))
        TS("dve", hbi[:], lbw[:, 32:44], 0.5, 0.5, ALU.mult, ALU.add, T_("lbw"), T_("hbi"))
        TS("dve", hnsc[:], lbw[:, 32:44], 0.5, -0.5, ALU.mult, ALU.add, T_("lbw"), T_("hnsc"))

        def xtok(which, t):
            return tk("%s_t%d" % (which, t))

        def tile_rows(ap, t):
            return ap[t * TT_:(t + 1) * TT_, :].rearrange("(b p) d -> p b d", p=128)

        xt3 = xt[:].rearrange("p (b d) -> p b d", b=2)
        xacc3 = xacc[:].rearrange("p (b d) -> p b d", b=2)
        xn3 = xn[:].rearrange("p (b d) -> p b d", b=2)
        hT3 = hT[:].rearrange("p (k t) -> p k t", k=8)
        V3 = Vsb[:].rearrange("p (b c) -> p b c", b=2)
        g3 = gts[:].rearrange("p (b d) -> p b d", b=2)
        o3 = o_sb[:].rearrange("p (b c) -> p b c", b=2)
        ptr3 = ptr[:].rearrange("p (k t) -> p k t", k=8)

        def ada_ln(l):
            DMA("sp", bada[:], b_ada_d[l], [], T_("bada"), "c_bada")
            DMA("sp", nmix[:], nmix_d[l], [], T_("nmix"), "c_nmix")
            DMA("sp", nmlp[:], nmlp_d[l], [], T_("nmlp"), "c_nmlp")
            for s in range(24):
                i = rr["wst"] % 2
                rr["wst"] += 1
                wv = wst[i][:].rearrange("p (k c) -> p k c", k=8)
                DMA("sp", wv, w_ada_d[l][:, s * 256:(s + 1) * 256].rearrange("(k p) c -> p k c", p=128), [], T_("wst%d" % i), "wst%d" % i)
                for j in range(2):
                    cc = s * 2 + j
                    for k in range(8):
                        MM(pX[:, cc:cc + 1], wv[:, k, j * 128:(j + 1) * 128], cond[:, k:k + 1], k == 0, k == 7,
                           T_("wst%d" % i, "cond"), T_("pX"))
                TTo("dve", modsb[:, s * 2:s * 2 + 2], pX[:, s * 2:s * 2 + 2], bada[:, s * 2:s * 2 + 2], ALU.add, T_("pX", "bada"), T_("modsb"))
            STT(g1[:], modsb[:, 8:16], 1.0, nmix[:], ALU.add, ALU.mult, T_("modsb", "nmix"), T_("g1"))
            STT(g2[:], modsb[:, 32:40], 1.0, nmlp[:], ALU.add, ALU.mult, T_("modsb", "nmlp"), T_("g2"))
            for which, base, dst, dname in ((0, 16, gate1_bc, "gate1_bc"), (1, 40, gate2_bc, "gate2_bc")):
                for half in range(2):
                    for f4 in range(4):
                        fc = half * 4 + f4
                        i = rr["diag"] % 2
                        rr["diag"] += 1
                        TS("dve", diag[i][:], ident_f[:], modsb[:, base + fc:base + fc + 1], None, ALU.mult, None,
                           T_("ident_f", "modsb"), T_("diag%d" % i))
                        MM(pX[:, f4 * 128:(f4 + 1) * 128], ones_f[:], diag[i][:], True, True, T_("ones_f", "diag%d" % i), T_("pX"))
                    CP("dve", dst[:, half * 512:(half + 1) * 512], pX[:, 0:512], T_("pX"), T_(dname))

        def layer_small(l):
            DMA("sp", gain_h[:], gainrow_d[l:l + 1, :].partition_broadcast(128), [], T_("gain_h"), "c_gain")
            TS("pool", gain_h[:], gain_h[:], 0.5, None, ALU.mult, None, T_("gain_h"), T_("gain_h"))
            DMA("sp", wgate_f[0:16, :], wgate_d[l], [], T_("wgate_f"), "c_wgate")
            CP("dve", wgate[0:16, :], wgate_f[0:16, :], T_("wgate_f"), T_("wgate"))
            DMA("sp", nbgate[0:96, :], bgate_d[l], [], T_("nbgate"), "c_bgate")
            TS("dve", nbgate[0:96, :], nbgate[0:96, :], -1.0, None, ALU.mult, None, T_("nbgate"), T_("nbgate"))
            DMA("sp", convw[:], convw_d[l].rearrange("p c k -> p (c k)"), [], T_("convw"), "c_convw")
            DMA("sp", mgb[0:8, :], mgb_d[l], [], T_("mgb"), "c_mgb")
            TS("dve", nmgb[0:8, :], mgb[0:8, :], -1.0, None, ALU.mult, None, T_("mgb"), T_("nmgb"))
            MS("pool", ebp[:], 1.0, T_("ebp"))
            for n_ in S:
                MS("pool", S[n_][:], 0.0, T_("S_" + n_))
            for i in range(4):
                MS("pool", convb[i][:, 0:3], 0.0, T_("convb%d" % i))

        def fence():
            P.add("pool", lambda e: e.nop(), [], [ARENA])

        def load_phaseA_weights(l):
            fence()
            for k in range(8):
                DMA("pool", wF[:, k, :], w_in_d[l][k * 128:(k + 1) * 128, 0:NFC], [ARENA], T_("wF%d" % k), "wF%d" % k)
                DMA("pool", wT[:, k, :], w_in_d[l][k * 128:(k + 1) * 128, NFC:NFC + NTC], [ARENA], T_("wT%d" % k), "wT%d" % k)
            for k in range(8):
                DMA("pool", wO[:, k, :], w_out_d[l][k * 128:(k + 1) * 128, :], [ARENA], T_("wO%d" % k), "wO%d" % k)

        def load_phaseB_weights(l, hf):
            fence()
            for k in range(8):
                DMA("pool", w1h[:, k, :], w_ff1_d[l][k * 128:(k + 1) * 128, hf * 2048:(hf + 1) * 2048], [ARENA], T_("w1h%d" % k), "w1h%d" % k)
            for k in range(16):
                r0 = hf * 2048 + k * 128
                DMA("pool", w2h[:, k, :], w_ff2_d[l][r0:r0 + 128, :], [ARENA], T_("w2h%d" % k), "w2h%d" % k)

        def norm_to_hT(gvec, shbase, gname):
            for bl in range(2):
                ACT(tmpY[:], xt3[:, bl, :], AF.Square, T_("xt"), T_("tmpY", "stat"), accum=stat[:, bl:bl + 1])
            ACT(stat[:, 2:4], stat[:, 0:2], AF.Ln, T_("stat", "cst"), T_("stat"), scale=1.0 / D, bias=C_EPS)
            ACT(stat[:, 4:6], stat[:, 2:4], AF.Exp, T_("stat"), T_("stat"), scale=-0.5)
            for bl in range(2):
                TS("dve", xn3[:, bl, :], xt3[:, bl, :], stat[:, 4 + bl:5 + bl], None, ALU.mult, None, T_("xt", "stat"), T_("xn"))
            for bl in range(2):
                for fc in range(8):
                    TR(ptr3[:, fc, 0:128], xn3[:, bl, fc * 128:(fc + 1) * 128], T_("xn"), T_("ptr"))
                for fc in range(8):
                    ACT(hT3[:, fc, bl * 128:(bl + 1) * 128], ptr3[:, fc, 0:128], AF.Identity, T_("ptr", gname, "modsb"), T_("hT"),
                        scale=gvec[:, fc:fc + 1], bias=modsb[:, shbase + fc:shbase + fc + 1])

        def fproj(name, dst_ps, dst_cols, pst):
            off, w = FCH[name]
            for k in range(8):
                MM(dst_ps[0:w, dst_cols[0]:dst_cols[1]], wF[:, k, off:off + w], hT3[:, k, :], k == 0, k == 7,
                   [ARENA, tk("wF%d" % k), tk("hT")], [pst])

        def core(nch, nh, kd, vw, vcol, Sname, bm, q_ap, q_r, k_ap, k_r, e1_ap, e1_r, e2_ap, e2_r, sc, scn):
            i = rr["ft"] % 2
            rr["ft"] += 1
            qt, kt, kk = qT[i], kT[i], ktok[i]
            qn, kn, kkn = "qT%d" % i, "kT%d" % i, "ktok%d" % i
            TTo("dve", qt[0:nch, :], q_ap, e1_ap, ALU.mult, q_r + e1_r, T_(qn))
            TTo("dve", kt[0:nch, :], k_ap, e2_ap, ALU.mult, k_r + e2_r, T_(kn))
            hmb = bm_gla if kd == 32 else bm_hg
            hmn = "bm_gla" if kd == 32 else "bm_hg"
            for h in range(nh):
                STT(qp[h][0:nch, :], q_ap, hmb[0:nch, h * 64:h * 64 + 1], e1_ap, ALU.mult, ALU.mult, q_r + e1_r + T_(hmn), T_("qp%d" % h))
            for bl in range(2):
                TR(ptr[:, bl * 128:bl * 128 + nch], kt[0:nch, bl * 128:(bl + 1) * 128], T_(kn), T_("ptr"))
            kk3 = kk[:].rearrange("p (b c) -> p b c", b=2)
            CP("act", kk3[:, :, 0:nch], ptr[:, 0:256].rearrange("p (b c) -> p b c", b=2)[:, :, 0:nch], T_("ptr"), T_(kkn))
            Sb = S[Sname]
            Sn = "S_" + Sname
            nv = nh * vw
            for bl in range(2):
                for c in range(2):
                    t0 = bl * 128 + c * 64
                    for h in range(nh):
                        MM(pP[c * 64:(c + 1) * 64, h * 64:(h + 1) * 64], kt[0:nch, t0:t0 + 64],
                           qp[h][0:nch, t0:t0 + 64], True, True, T_(kn, "qp%d" % h), T_("pP"))
                for c in range(2):
                    MM(pUs[c][0:nch, 0:nv], kk3[c * 64:(c + 1) * 64, bl, 0:nch], V3[c * 64:(c + 1) * 64, bl, vcol:vcol + nv], True, True,
                       T_(kkn, "Vsb"), T_("pU%d" % c))
                ip = rr["pt"] % 2
                rr["pt"] += 1
                ptb, ptn = PT[ip], "PT%d" % ip
                P.add("dve", lambda e, ptb=ptb: e.copy_predicated(out=ptb[:, 0:nh * 64], mask=maskp[:, 0:nh * 64], data=pP[:, 0:nh * 64]),
                      T_("pP", "maskp"), T_(ptn))
                for c in range(2):
                    ci = bl * 2 + c
                    t0 = bl * 128 + c * 64
                    isr = rr["sr"] % 4
                    rr["sr"] += 1
                    srb, srn = Sr[isr], "Sr%d" % isr
                    STT(srb[0:nch, 0:nv], Sb[0:nch, 0:nv], sc[0:nch, 4 + ci:5 + ci], bm[0:nch, 0:nv], ALU.mult, ALU.mult,
                        T_(Sn, scn), T_(srn))
                    MM(pO[c * 64:(c + 1) * 64, 0:nv], qt[0:nch, t0:t0 + 64], srb[0:nch, 0:nv], True, False, T_(qn, srn), T_("pO"))
                    for h in range(nh):
                        MM(pO[c * 64:(c + 1) * 64, h * vw:(h + 1) * vw], ptb[c * 64:(c + 1) * 64, h * 64:(h + 1) * 64],
                           V3[c * 64:(c + 1) * 64, bl, vcol + h * vw:vcol + (h + 1) * vw], False, h == nh - 1, T_(ptn, "Vsb"), T_("pO"))
                    STT(Sb[0:nch, 0:nv], Sb[0:nch, 0:nv], sc[0:nch, 4 + ci:5 + ci], pUs[c][0:nch, 0:nv], ALU.mult, ALU.add,
                        T_("pU%d" % c, Sn, scn), T_(Sn))
                CP("act", o3[:, bl, vcol:vcol + nv], pO[:, 0:nv], T_("pO"), T_("o_sb"))

        def chunk_scalars(nch, sA, sc, scn, Sname):
            b3 = bF[0:nch, :].rearrange("p (c j) -> p c j", j=64)
            d3 = dF[0:nch, :].rearrange("p (c j) -> p c j", j=64)
            ACT(sc[0:nch, 0:4], b3[:, :, 31], AF.Exp, T_("bF"), T_(scn), scale=sA)
            ACT(sc[0:nch, 8:12], d3[:, :, 63], AF.Exp, T_("dF"), T_(scn), scale=sA)
            make_f(nch, sc, scn, Sname)

        def make_f(nch, sc, scn, Sname):
            e = EBP[Sname]
            TTo("dve", sc[0:nch, 4:5], ebp[0:nch, e:e + 1], sc[0:nch, 0:1], ALU.mult, T_("ebp", scn), T_(scn))
            TTo("dve", sc[0:nch, 5:8], sc[0:nch, 8:11], sc[0:nch, 1:4], ALU.mult, T_(scn), T_(scn))
            CP("dve", ebp[0:nch, e:e + 1], sc[0:nch, 11:12], T_(scn), T_("ebp"))

        def scan_and_exps(nch, sA, biasq, biask, sc, scn, Sname):
            P.add("dve", lambda e: e.tensor_tensor_scan(out=bF[0:nch, :], data0=rmask[0:nch, :], data1=gF[0:nch, :], initial=0.0,
                                                         op0=ALU.mult, op1=ALU.add), T_("rmask", "gF"), T_("bF"))
            b3 = bF[0:nch, :].rearrange("p (c j) -> p c j", j=64)
            d3 = dF[0:nch, :].rearrange("p (c j) -> p c j", j=64)
            TTo("dve", d3, b3, b3[:, :, 31:32].to_broadcast([nch, 4, 64]), ALU.subtract, T_("bF"), T_("dF"))
            ACT(E1[0:nch, :], dF[0:nch, :], AF.Exp, T_("dF", "cst"), T_("E1"), scale=sA, bias=biasq[0:nch, :])
            ACT(E2[0:nch, :], dF[0:nch, :], AF.Exp, T_("dF", "cst"), T_("E2"), scale=-sA, bias=biask[0:nch, :])
            chunk_scalars(nch, sA, sc, scn, Sname)

        def phaseA_tile(l, t, src_d, src_tok):
            DMA("sp", xt3, tile_rows(src_d, t), [src_tok], T_("xt"), "xt")
            norm_to_hT(g1, 0, "g1")
            for bl in range(2):
                for cg in range(4):
                    ps, pst = next_mm()
                    for k in range(8):
                        MM(ps[:, 0:512], hT3[:, k, bl * 128:(bl + 1) * 128], wT[:, k, cg * 512:(cg + 1) * 512], k == 0, k == 7,
                           [ARENA, tk("wT%d" % k), tk("hT")], [pst])
                    if cg == 0:
                        CP("act", V3[:, bl, 0:512], ps[:, 0:512], [pst], T_("Vsb"))
                    elif cg == 1:
                        CP("act", V3[:, bl, 512:768], ps[:, 0:256], [pst], T_("Vsb"))
                        CP("act", V3[:, bl, 768:1028].rearrange("p (h c) -> p h c", h=4)[:, :, 0:64],
                           ps[:, 256:512].rearrange("p (h c) -> p h c", h=4), [pst], T_("Vsb"))
                    else:
                        c0 = (cg - 2) * 512
                        ACT(bufA[:, 0:512], ps[:, 0:512], AF.Tanh, [pst], T_("bufA"), scale=0.5)
                        if cg == 2:
                            STT(bufA[:, 0:384], bufA[:, 0:384], 1.0, ps[:, 0:384], ALU.add, ALU.mult, [tk("bufA"), pst], T_("bufA"))
                            TTo("pool", g3[:, bl, 0:384], bufA[:, 0:384], gain_h[:, 0:384], ALU.mult, T_("bufA", "gain_h"), T_("gts"))
                            STT(g3[:, bl, 384:512], bufA[:, 384:512], 1.0, gain_h[:, 384:512], ALU.add, ALU.mult, T_("bufA", "gain_h"), T_("gts"))
                        else:
                            STT(g3[:, bl, 512:1024], bufA[:, 0:512], 1.0, gain_h[:, 512:1024], ALU.add, ALU.mult, T_("bufA", "gain_h"), T_("gts"))
            ps, pst = next_mm()
            fproj("glow", ps, (0, TT_), pst)
            CP("act", glowT[0:16, :], ps[0:16, 0:TT_], [pst], T_("glowT"))
            for g in range(2):
                ps, pst = next_mm()
                fproj("gq%d" % g, ps, (0, TT_), pst)
                fproj("gk%d" % g, ps, (TT_, 2 * TT_), pst)
                MM(pX[0:96, 0:TT_], wgate[0:16, g * 96:(g + 1) * 96], glowT[0:16, :], True, True, T_("wgate", "glowT"), T_("pX"))
                ACT(th[0:96, :], pX[0:96, 0:TT_], AF.Exp, T_("pX", "nbgate"), T_("th"), scale=-1.0, bias=nbgate[0:96, g:g + 1])
                ACT(gF[0:96, :], th[0:96, :], AF.Ln, T_("th", "cst"), T_("gF"), scale=1.0, bias=C_ONE[0:96, :])
                sc = csc[rr["ft"] % 2]; scn = "csc%d" % (rr["ft"] % 2)
                scan_and_exps(96, -1.0 / 16.0, C_LNQ, C_ZERO, sc, scn, "g%d" % g)
                core(96, 3, 32, 64, g * 192, "g%d" % g, bm_gla, ps[0:96, 0:TT_], [pst], ps[0:96, TT_:2 * TT_], [pst],
                     E1[0:96, :], T_("E1"), E2[0:96, :], T_("E2"), sc, scn)
            for h_ in range(3):
                ps, pst = next_mm()
                fproj("hq%d" % h_, ps, (0, TT_), pst)
                fproj("hf%d" % h_, ps, (TT_, 2 * TT_), pst)
                li = l * 3 + h_
                ACT(th[:, :], ps[:, TT_:2 * TT_], AF.Tanh, [pst], T_("th"), scale=0.5)
                ACT(gF[:, :], th[:, :], AF.Ln, T_("th", "hsc", "hbi"), T_("gF"), scale=hsc[:, li:li + 1], bias=hbi[:, li:li + 1])
                TS("pool", ksrc[:, :], th[:, :], hnsc[:, li:li + 1], hsc[:, li:li + 1], ALU.mult, ALU.add, T_("th", "hnsc", "hsc"), T_("ksrc"))
                ACT(th[:, :], ps[:, 0:TT_], AF.Tanh, [pst, tk("ksrc"), tk("gF")], T_("th"), scale=0.5)
                STT(qsrc[:, :], th[:, :], 1.0, ps[:, 0:TT_], ALU.add, ALU.mult, [tk("th"), pst], T_("qsrc"))
                sc = csc[rr["ft"] % 2]; scn = "csc%d" % (rr["ft"] % 2)
                scan_and_exps(128, 1.0, C_LNH, C_ZERO, sc, scn, "h%d" % h_)
                core(128, 2, 64, 64, 384 + h_ * 128, "h%d" % h_, bm_hg, qsrc[:, :], T_("qsrc"), ksrc[:, :], T_("ksrc"),
                     E1[:, :], T_("E1"), E2[:, :], T_("E2"), sc, scn)
            ps, pst = next_mm()
            fproj("mimf", ps, (0, TT_), pst)
            ACT(x8[0:8, :], ps[0:8, 0:TT_], AF.Identity, [pst, tk("mgb")], T_("x8"), scale=1.0, bias=mgb[0:8, :])
            ACT(l8[0:8, :], ps[0:8, 0:TT_], AF.Exp, [pst, tk("nmgb")], T_("l8"), scale=-1.0, bias=nmgb[0:8, :])
            ACT(l8[0:8, :], l8[0:8, :], AF.Ln, T_("l8", "cst"), T_("l8"), scale=1.0, bias=C_ONE[0:8, :])
            P.add("dve", lambda e: e.tensor_tensor_scan(out=b8[0:8, :], data0=rmask[0:8, :], data1=l8[0:8, :], initial=0.0,
                                                         op0=ALU.mult, op1=ALU.add), T_("rmask", "l8"), T_("b8"))
            b83 = b8[0:8, :].rearrange("p (c j) -> p c j", j=64)
            d83 = d8[0:8, :].rearrange("p (c j) -> p c j", j=64)
            TTo("dve", d83, b83, b83[:, :, 31:32].to_broadcast([8, 4, 64]), ALU.subtract, T_("b8"), T_("d8"))
            CP("dve", cs8[0:8, 0:4], b83[:, :, 31], T_("b8"), T_("cs8"))
            CP("dve", cs8[0:8, 4:8], b83[:, :, 63], T_("b8"), T_("cs8"))
            CP("dve", cs8[0:8, 8:12], d83[:, :, 63], T_("d8"), T_("cs8"))
            sel3 = sel[0:8, :].rearrange("r (s m) -> r s m", s=6)
            for m_ in range(2):
                srcs = []
                for qi, nm in ((0, "mq%d" % m_), (1, "mk%d" % m_)):
                    ps, pst = next_mm()
                    fproj(nm, ps, (0, TT_), pst)
                    ci_ = qi * 2 + m_
                    cb, cbn = convb[ci_], "convb%d" % ci_
                    CP("act", cb[:, 3:3 + TT_], ps[:, 0:TT_], [pst], T_(cbn))
                    dst = qsrc if qi == 0 else ksrc
                    dn = "qsrc" if qi == 0 else "ksrc"
                    TS("dve", dst[:, :], cb[:, 0:TT_], convw[:, ci_ * 4:ci_ * 4 + 1], None, ALU.mult, None, T_(cbn, "convw"), T_(dn))
                    for k_ in range(1, 4):
                        STT(dst[:, :], cb[:, k_:k_ + TT_], convw[:, ci_ * 4 + k_:ci_ * 4 + k_ + 1], dst[:, :], ALU.mult, ALU.add,
                            T_(cbn, "convw", dn), T_(dn))
                    CP("pool", cb[:, 0:3], cb[:, TT_:TT_ + 3], T_(cbn), T_(cbn))
                    ACT(th[:, :], dst[:, :], AF.Tanh, T_(dn), T_("th"), scale=0.5)
                    STT(dst[:, :], th[:, :], 1.0, dst[:, :], ALU.add, ALU.mult, T_("th", dn), T_(dn))
                MM(pX[:, 0:TT_], sel3[:, m_, :], d8[0:8, :], True, True, T_("sel", "d8"), T_("pX"))
                ACT(E1[:, :], pX[:, 0:TT_], AF.Exp, T_("pX", "cst"), T_("E1"), scale=1.0, bias=C_LNH)
                MM(pX[:, 0:TT_], sel3[:, 2 + m_, :], d8[0:8, :], True, False, T_("sel", "d8"), T_("pX"))
                MM(pX[:, 0:TT_], sel3[:, 4 + m_, :], x8[0:8, :], False, True, T_("sel", "x8"), T_("pX"))
                ACT(E2[:, :], pX[:, 0:TT_], AF.Exp, T_("pX", "cst"), T_("E2"), scale=1.0, bias=C_LNK)
                sc = csc[rr["ft"] % 2]; scn = "csc%d" % (rr["ft"] % 2)
                MM(pX[:, 0:12], sel3[:, m_, :], cs8[0:8, 0:12], True, True, T_("sel", "cs8"), T_("pX"))
                ACT(sc[:, 0:12], pX[:, 0:12], AF.Exp, T_("pX"), T_(scn))
                make_f(128, sc, scn, "m%d" % m_)
                core(128, 2, 64, 65, 768 + m_ * 130, "m%d" % m_, bm_ml, qsrc[:, :], T_("qsrc"), ksrc[:, :], T_("ksrc"),
                     E1[:, :], T_("E1"), E2[:, :], T_("E2"), sc, scn)
            for bl in range(2):
                ogh = o3[:, bl, 0:768]
                ACT(bufA[:, 0:768], ogh, AF.Square, T_("o_sb"), T_("bufA"))
                RED(stat[:, 8:20], bufA[:, 0:768].rearrange("p (h c) -> p h c", c=64), T_("bufA"), T_("stat"))
                oml = o3[:, bl, 768:1028].rearrange("p (h c) -> p h c", c=65)
                ACT(stat[:, 32:36], oml[:, :, 64], AF.Abs, T_("o_sb"), T_("stat"))
                TS("dve", stat[:, 32:36], stat[:, 32:36], 1.0, None, ALU.max, None, T_("stat"), T_("stat"))
                P.add("dve", lambda e: e.reciprocal(out=stat[:, 36:40], in_=stat[:, 32:36]), T_("stat"), T_("stat"))
                hml = bufB[:, 0:256].rearrange("p (h c) -> p h c", c=64)
                TTo("dve", hml, oml[:, :, 0:64], stat[:, 36:40].unsqueeze(2).to_broadcast([128, 4, 64]), ALU.mult, T_("o_sb", "stat"), T_("bufB"))
                RED(stat[:, 40:44], hml, T_("bufB"), T_("stat"))
                TS("dve", stat[:, 40:44], stat[:, 40:44], -1.0 / 64.0, None, ALU.mult, None, T_("stat"), T_("stat"))
                TTo("dve", hml, hml, stat[:, 40:44].unsqueeze(2).to_broadcast([128, 4, 64]), ALU.add, T_("bufB", "stat"), T_("bufB"))
                ACT(bufA[:, 768:1024], bufB[:, 0:256], AF.Square, T_("bufB"), T_("bufA"))
                RED(stat[:, 20:24], bufA[:, 768:1024].rearrange("p (h c) -> p h c", c=64), T_("bufA"), T_("stat"))
                ACT(stat[:, 8:24], stat[:, 8:24], AF.Ln, T_("stat", "cst"), T_("stat"), scale=1.0 / 64.0, bias=C_EPS)
                ACT(stat[:, 8:24], stat[:, 8:24], AF.Exp, T_("stat"), T_("stat"), scale=-0.5)
                TTo("dve", bufA[:, 0:768].rearrange("p (h c) -> p h c", c=64), ogh.rearrange("p (h c) -> p h c", c=64),
                    stat[:, 8:20].unsqueeze(2).to_broadcast([128, 12, 64]), ALU.mult, T_("o_sb", "stat"), T_("bufA"))
                TTo("dve", bufA[:, 768:1024].rearrange("p (h c) -> p h c", c=64), hml,
                    stat[:, 20:24].unsqueeze(2).to_broadcast([128, 4, 64]), ALU.mult, T_("bufB", "stat"), T_("bufA"))
                TTo("pool", xn3[:, bl, :], bufA[:, :], g3[:, bl, :], ALU.mult, T_("bufA", "gts"), T_("xn"))
            for bl in range(2):
                for fc in range(8):
                    TR(ptr3[:, fc, 0:128], xn3[:, bl, fc * 128:(fc + 1) * 128], T_("xn"), T_("ptr"))
                CP("act", hT3[:, :, bl * 128:(bl + 1) * 128], ptr3[:, :, 0:128], T_("ptr"), T_("hT"))
            for bl in range(2):
                for cg in range(2):
                    ps, pst = next_mm()
                    for k in range(8):
                        MM(ps[:, 0:512], hT3[:, k, bl * 128:(bl + 1) * 128], wO[:, k, cg * 512:(cg + 1) * 512], k == 0, k == 7,
                           [ARENA, tk("wO%d" % k), tk("hT")], [pst])
                    TTo("dve", tmpY[:, cg * 512:(cg + 1) * 512], ps[:, 0:512], gate1_bc[:, cg * 512:(cg + 1) * 512], ALU.mult,
                        [pst, tk("gate1_bc")], T_("tmpY"))
                    TTo("pool", xt3[:, bl, cg * 512:(cg + 1) * 512], xt3[:, bl, cg * 512:(cg + 1) * 512], tmpY[:, cg * 512:(cg + 1) * 512],
                        ALU.add, T_("xt", "tmpY"), T_("xt"))
            DMA("sp", tile_rows(xb_d, t), xt3, T_("xt"), [xtok("xb", t)], "xst")

        def phaseB_tile(l, t, hf, last):
            DMA("sp", xt3, tile_rows(xb_d, t), [xtok("xb", t)], T_("xt"), "xt")
            if hf == 1:
                DMA("sp", xacc3, tile_rows(xa_d, t), [xtok("xa", t)], T_("xacc"), "xacc")
            norm_to_hT(g2, 24, "g2")
            aT_A = bufA[:].bitcast(BF16).rearrange("p (k t) -> p k t", k=8)
            aT_B = bufB[:].bitcast(BF16).rearrange("p (k t) -> p k t", k=8)
            for pr in range(8):
                ps, pst = next_mm()
                for j in range(2):
                    ffc = pr * 2 + j
                    for k in range(8):
                        MM(ps[:, j * TT_:(j + 1) * TT_], w1h[:, k, ffc * 128:(ffc + 1) * 128], hT3[:, k, :], k == 0, k == 7,
                           [ARENA, tk("w1h%d" % k), tk("hT")], [pst])
                ACT(tmpY[:, 0:512], ps[:, 0:512], AF.Relu, [pst], T_("tmpY"))
                if pr < 4:
                    dstv, dn = aT_A[:, pr * 2:pr * 2 + 2, :], "bufA"
                else:
                    dstv, dn = aT_B[:, (pr - 4) * 2:(pr - 4) * 2 + 2, :], "bufB"
                TTo("pool", dstv, tmpY[:, 0:512].rearrange("p (k t) -> p k t", k=2), tmpY[:, 0:512].rearrange("p (k t) -> p k t", k=2),
                    ALU.mult, T_("tmpY"), T_(dn))
            base = xt3 if hf == 0 else xacc3
            bn = "xt" if hf == 0 else "xacc"
            for bl in range(2):
                for cg in range(2):
                    ps, pst = next_mm()
                    for ffc in range(16):
                        av, an = (aT_A, "bufA") if ffc < 8 else (aT_B, "bufB")
                        MM(ps[:, 0:512], av[:, ffc % 8, bl * 128:(bl + 1) * 128], w2h[:, ffc, cg * 512:(cg + 1) * 512], ffc == 0, ffc == 15,
                           [ARENA, tk("w2h%d" % ffc), tk(an)], [pst])
                    TTo("dve", tmpY[:, cg * 512:(cg + 1) * 512], ps[:, 0:512], gate2_bc[:, cg * 512:(cg + 1) * 512], ALU.mult,
                        [pst, tk("gate2_bc")], T_("tmpY"))
                    TTo("pool", base[:, bl, cg * 512:(cg + 1) * 512], base[:, bl, cg * 512:(cg + 1) * 512], tmpY[:, cg * 512:(cg + 1) * 512],
                        ALU.add, T_(bn, "tmpY"), T_(bn))
            if not last:
                DMA("sp", tile_rows(xa_d, t), base, T_(bn), [xtok("xa", t)], "xst")
            else:
                for bl in range(2):
                    ACT(tmpY[:], base[:, bl, :], AF.Square, T_(bn), T_("tmpY", "stat"), accum=stat[:, bl:bl + 1])
                ACT(stat[:, 2:4], stat[:, 0:2], AF.Ln, T_("stat", "cst"), T_("stat"), scale=1.0 / D, bias=C_EPS)
                ACT(stat[:, 4:6], stat[:, 2:4], AF.Exp, T_("stat"), T_("stat"), scale=-0.5)
                for bl in range(2):
                    STT(base[:, bl, :], base[:, bl, :], stat[:, 4 + bl:5 + bl], gain_h[:, :], ALU.mult, ALU.mult, T_(bn, "stat", "gain_h"), T_(bn))
                DMA("sp", tile_rows(out_d, t), base, T_(bn), [xtok("out", t)], "xst")

        import os
        KSTOP = os.environ.get("KSTOP", "")
        for l in range(nlayers):
            if KSTOP == "setup":
                break
            ada_ln(l)
            if KSTOP == "ada":
                break
            layer_small(l)
            if KSTOP == "small":
                break
            load_phaseA_weights(l)
            if KSTOP == "w":
                break
            for t in range(NT):
                if KSTOP.startswith("A") and t >= int(KSTOP[1:]):
                    break
                if l == 0:
                    phaseA_tile(l, t, x_d, tk("x_in"))
                else:
                    phaseA_tile(l, t, xa_d, xtok("xa", t))
            if KSTOP.startswith("A"):
                break
            lastl = (l == nlayers - 1)
            for hf in range(2):
                load_phaseB_weights(l, hf)
                if lastl and hf == 1:
                    DMA("sp", gain_h[:], fnorm_d[0:1, :].partition_broadcast(128), [], T_("gain_h"), "c_gain")
                for t in range(NT):
                    phaseB_tile(l, t, hf, lastl and hf == 1)
        fin = P.add("sp", lambda e: e.nop(), [xtok("out", t) for t in range(NT)], [])
        P.finalize_and_emit()
    return nc


def kernel(**inputs):
    com = host_prep(inputs)
    x = np.asarray(inputs["x"], np.float32)
    c = np.asarray(inputs["c"], np.float32)
    nb = x.shape[0]
    nc = build_program()
    in_maps = []
    for b in range(nb):
        m = dict(com)
        m["x"] = np.ascontiguousarray(x[b])
        m["cT"] = np.ascontiguousarray(c[b].reshape(8, 128).T)
        in_maps.append(m)
    res = run_bass_kernel_spmd(nc, in_maps, core_ids=list(range(nb)))
    out = np.stack([np.asarray(res.results[b]["out"], np.float32) for b in range(nb)], axis=0)
    return out
```
